# Optimizing a Trainium2 kernel written in Bass

```python
import jax, jax.numpy as jnp
from jax import lax
import numpy as np

D_MODEL = 2048
BATCH = 8
SEQ = 2048
DEPTH = 1

D_MIX = D_MODEL
A_HEADS = 8
A_HEAD_DIM = 128
A_WIDTH = A_HEADS * A_HEAD_DIM
DILATED_PATTERNS = ((128, 1), (512, 4), (2048, 16))
B_HEADS = 8
B_V_DIM = 128
B_WIDTH = B_HEADS * B_V_DIM
B_NOPE_DIM = 128
B_ROPE_DIM = 64
Q_LORA_RANK = 512
KV_LORA_RANK = 256
ROPE_THETA = 10000.0
NORM_EPS = 1e-6
Q_BLOCK = 128
NEG_INF = -1e30
IN_SIZES = (A_WIDTH, A_WIDTH, A_WIDTH, A_WIDTH, Q_LORA_RANK, KV_LORA_RANK, B_ROPE_DIM, B_WIDTH)
D_IN = A_WIDTH * 4 + Q_LORA_RANK + KV_LORA_RANK + B_ROPE_DIM + B_WIDTH

kernel_name = 'hymba_dilated_mla_adaln_block'


def rms_norm(x, g):
    xf = x.astype(jnp.float32)
    y = xf * lax.rsqrt(jnp.mean(xf * xf, axis=-1, keepdims=True) + NORM_EPS)
    return (y * g.astype(jnp.float32)).astype(x.dtype)


def alibi_slopes(n_heads):
    return jnp.exp2(-8.0 * jnp.arange(1, n_heads + 1, dtype=jnp.float32) / n_heads)


def banded_attention(q, k, v, radius, dist_slope):
    *lead, L, hd = q.shape
    nlead = len(lead)
    blk = radius
    nb = -(-L // blk)
    lp = nb * blk
    qb = jnp.pad(q, [(0, 0)] * nlead + [(0, lp - L), (0, 0)]).reshape(*lead, nb, blk, hd)

    def windows(a):
        ap = jnp.pad(a, [(0, 0)] * nlead + [(blk, lp - L + blk), (0, 0)]).reshape(*lead, nb + 2, blk, hd)
        return jnp.concatenate([ap[..., :-2, :, :], ap[..., 1:-1, :, :], ap[..., 2:, :, :]], axis=-2)

    kw, vw = windows(k), windows(v)
    s = jnp.einsum('...nqd,...nkd->...nqk', qb, kw).astype(jnp.float32) * (hd ** -0.5)
    qpos = jnp.arange(lp).reshape(nb, blk)
    kpos = qpos[:, :1] - blk + jnp.arange(3 * blk)[None, :]
    rel = kpos[:, None, :] - qpos[:, :, None]
    valid = (jnp.abs(rel) <= radius) & (kpos >= 0)[:, None, :] & (kpos < L)[:, None, :]
    s = jnp.where(valid, s - dist_slope * jnp.abs(rel).astype(jnp.float32), NEG_INF)
    m = jnp.max(s, axis=-1, keepdims=True)
    p = jnp.exp(s - m)
    l = jnp.sum(p, axis=-1, keepdims=True)
    o = jnp.einsum('...nqk,...nkd->...nqd', p, vw.astype(jnp.float32)) / l
    lse = (m + jnp.log(l))[..., 0]
    o = o.reshape(*lead, lp, hd)[..., :L, :]
    lse = lse.reshape(*lead, lp)[..., :L]
    return o, lse


def dilated_mixture(q, k, v):
    B, T, H, hd = q.shape
    slopes = alibi_slopes(H)
    outs, lses = [], []
    for window, dil in DILATED_PATTERNS:
        radius = window // 2 // dil
        L = T // dil

        def to_strided(a):
            return a.reshape(B, L, dil, H, hd).transpose(0, 2, 3, 1, 4)

        o, lse = banded_attention(to_strided(q), to_strided(k), to_strided(v), radius,
                                  (slopes * dil)[:, None, None, None])
        outs.append(o.transpose(0, 3, 1, 2, 4).reshape(B, T, H, hd))
        lses.append(lse.transpose(0, 3, 1, 2).reshape(B, T, H))
    w = jax.nn.softmax(jnp.stack(lses, axis=0), axis=0)
    out = jnp.einsum('pbth,pbthd->bthd', w, jnp.stack(outs, axis=0))
    return out.astype(q.dtype)


def apply_rope(x, cos, sin):
    x1, x2 = jnp.split(x, 2, axis=-1)
    return jnp.concatenate([x1 * cos - x2 * sin, x1 * sin + x2 * cos], axis=-1)


def dense_attention(q, k, v):
    B, T, H, dk = q.shape
    nq = T // Q_BLOCK
    qb = q.reshape(B, nq, Q_BLOCK, H, dk).transpose(1, 0, 2, 3, 4)
    scale = dk ** -0.5

    def one_block(qi):
        s = jnp.einsum('bqhd,bkhd->bhqk', qi, k).astype(jnp.float32) * scale
        p = jax.nn.softmax(s, axis=-1)
        return jnp.einsum('bhqk,bkhd->bqhd', p.astype(v.dtype), v)

    o = lax.map(one_block, qb)
    return o.transpose(1, 0, 2, 3, 4).reshape(B, T, H, v.shape[-1])


def mla(cq, ckv, kpe, g_q_lora, w_uq, g_kv_lora, w_ukv):
    B, T, _ = cq.shape
    q = (rms_norm(cq, g_q_lora) @ w_uq).reshape(B, T, B_HEADS, B_NOPE_DIM + B_ROPE_DIM)
    q_nope, q_pe = q[..., :B_NOPE_DIM], q[..., B_NOPE_DIM:]
    kv = (rms_norm(ckv, g_kv_lora) @ w_ukv).reshape(B, T, B_HEADS, B_NOPE_DIM + B_V_DIM)
    k_nope, v = kv[..., :B_NOPE_DIM], kv[..., B_NOPE_DIM:]
    half = B_ROPE_DIM // 2
    pos = jnp.arange(T, dtype=jnp.float32)
    inv_freq = jnp.power(ROPE_THETA, -jnp.arange(half, dtype=jnp.float32) / half)
    ang = pos[:, None] * inv_freq[None, :]
    cos, sin = jnp.cos(ang).astype(cq.dtype), jnp.sin(ang).astype(cq.dtype)
    q_pe = apply_rope(q_pe, cos[None, :, None, :], sin[None, :, None, :])
    k_pe = apply_rope(kpe, cos[None], sin[None])
    k_pe = jnp.broadcast_to(k_pe[:, :, None, :], (B, T, B_HEADS, B_ROPE_DIM))
    q_full = jnp.concatenate([q_nope, q_pe], axis=-1)
    k_full = jnp.concatenate([k_nope, k_pe], axis=-1)
    return dense_attention(q_full, k_full, v)


def hybrid_layer(x, c, w_ada, b_ada, g_pre, w_in, g_q_lora, w_uq, g_kv_lora, w_ukv, w_out, g_post):
    B, T, _ = x.shape
    mod = jax.nn.silu(c) @ w_ada + b_ada
    shift, scale, gate = jnp.split(mod, 3, axis=-1)
    h = rms_norm(x, g_pre) * (1.0 + scale[:, None, :]) + shift[:, None, :]
    proj = h @ w_in
    split_points = [int(s) for s in np.cumsum(IN_SIZES)[:-1]]
    a_q, a_k, a_v, a_z, b_cq, b_ckv, b_kpe, b_z = jnp.split(proj, split_points, axis=-1)
    head_shape = (B, T, A_HEADS, A_HEAD_DIM)
    y_a = dilated_mixture(a_q.reshape(head_shape), a_k.reshape(head_shape), a_v.reshape(head_shape))
    y_a = y_a.reshape(B, T, A_WIDTH) * jax.nn.silu(a_z)
    y_b = mla(b_cq, b_ckv, b_kpe, g_q_lora, w_uq, g_kv_lora, w_ukv).reshape(B, T, B_WIDTH) * jax.nn.silu(b_z)
    y = jnp.concatenate([y_a, y_b], axis=-1) @ w_out
    return x + gate[:, None, :] * rms_norm(y, g_post)


def setup_inputs(seed: int = 0) -> dict:
    key = jax.random.key(seed)
    ks = jax.random.split(key, 14)
    f32 = jnp.float32

    def nrm(k, shape, s):
        return jax.random.normal(k, shape, f32) * s

    return {
        'x': nrm(ks[0], (BATCH, SEQ, D_MODEL), 1.0),
        'c': nrm(ks[1], (BATCH, D_MODEL), 1.0),
        'w_ada': nrm(ks[2], (DEPTH, D_MODEL, 3 * D_MODEL), 0.5 * D_MODEL ** -0.5),
        'b_ada': nrm(ks[3], (DEPTH, 3 * D_MODEL), 0.01),
        'g_pre': 1.0 + nrm(ks[4], (DEPTH, D_MODEL), 0.01),
        'w_in': nrm(ks[5], (DEPTH, D_MODEL, D_IN), D_MODEL ** -0.5),
        'g_q_lora': 1.0 + nrm(ks[6], (DEPTH, Q_LORA_RANK), 0.01),
        'w_uq': nrm(ks[7], (DEPTH, Q_LORA_RANK, B_HEADS * (B_NOPE_DIM + B_ROPE_DIM)), Q_LORA_RANK ** -0.5),
        'g_kv_lora': 1.0 + nrm(ks[8], (DEPTH, KV_LORA_RANK), 0.01),
        'w_ukv': nrm(ks[9], (DEPTH, KV_LORA_RANK, B_HEADS * (B_NOPE_DIM + B_V_DIM)), KV_LORA_RANK ** -0.5),
        'w_out': nrm(ks[10], (DEPTH, D_MIX, D_MODEL), D_MIX ** -0.5),
        'g_post': 1.0 + nrm(ks[11], (DEPTH, D_MODEL), 0.01),
    }


def reference(x, c, w_ada, b_ada, g_pre, w_in, g_q_lora, w_uq, g_kv_lora, w_ukv, w_out, g_post):
    for layer in range(DEPTH):
        x = hybrid_layer(x, c, w_ada[layer], b_ada[layer], g_pre[layer], w_in[layer],
                         g_q_lora[layer], w_uq[layer], g_kv_lora[layer], w_ukv[layer],
                         w_out[layer], g_post[layer])
    return x
```

```python
import numpy as np
from contextlib import ExitStack
import concourse.bass as bass
import concourse.mybir as mybir
from concourse.bass_utils import run_bass_kernel_spmd

F32 = mybir.dt.float32
BF16 = mybir.dt.bfloat16
AF = mybir.ActivationFunctionType
ALU = mybir.AluOpType
AX = mybir.AxisListType

T = 2048
D = 2048
NCH = 48
EPS = 1e-6
SC_A = 128.0 ** -0.5
SC_B = 192.0 ** -0.5
NEG = -30000.0

ENGS = ("pe", "act", "dve", "pool", "sp")


class Op:
    __slots__ = ("eng", "fn", "waits", "signal", "semval", "idx", "dma_sem", "dma_val")

    def __init__(self, eng, fn):
        self.eng = eng
        self.fn = fn
        self.waits = []
        self.signal = False
        self.semval = None
        self.idx = None
        self.dma_sem = None
        self.dma_val = None


class Prog:
    def __init__(self):
        self.q = {e: [] for e in ENGS}
        self.res = {}
        self.waited = {e: {} for e in ENGS}
        self.dma_cnt = {}
        self.dma_names = []
        self.pending_barrier = {e: [] for e in ENGS}

    def _need(self, op, prod):
        if prod is None:
            return
        if prod.dma_sem is not None:
            key = ("dma", prod.dma_sem)
            val = prod.dma_val
        else:
            if prod.eng == op.eng and op.eng in ("pe", "sp"):
                return
            key = ("eng", prod.eng)
            val = prod.idx
        if self.waited[op.eng].get(key, -1) >= val:
            return
        self.waited[op.eng][key] = val
        prod.signal = True
        op.waits.append(prod)

    def barrier(self):
        lasts = []
        for e in ENGS:
            if e == "sp":
                continue
            for o in reversed(self.q[e]):
                if o.dma_sem is None:
                    lasts.append(o)
                    break
        dmas = {}
        for e in ENGS:
            for o in self.q[e]:
                if o.dma_sem is not None:
                    dmas[o.dma_sem] = o
        lasts += list(dmas.values())
        for e in ENGS:
            self.pending_barrier[e] = list(lasts)

    def op(self, eng, fn, reads=(), writes=(), dma_sem=None):
        o = Op(eng, fn)
        if self.pending_barrier[eng]:
            for pr in self.pending_barrier[eng]:
                self._need(o, pr)
            self.pending_barrier[eng] = []
        for k in reads:
            r = self.res.get(k)
            if r is not None:
                self._need(o, r[0])
        for k in writes:
            r = self.res.get(k)
            if r is not None:
                self._need(o, r[0])
                for rd in r[1]:
                    self._need(o, rd)
        o.idx = len(self.q[eng])
        self.q[eng].append(o)
        if dma_sem is not None:
            if dma_sem not in self.dma_cnt:
                self.dma_names.append(dma_sem)
            o.dma_sem = dma_sem
            self.dma_cnt[dma_sem] = self.dma_cnt.get(dma_sem, 0) + 16
            o.dma_val = self.dma_cnt[dma_sem]
        for k in reads:
            r = self.res.get(k)
            if r is None:
                self.res[k] = [None, [o]]
            else:
                r[1].append(o)
        for k in writes:
            self.res[k] = [o, []]
        return o

    def wait_only(self, eng, prods):
        o = Op(eng, lambda e: None)
        for pr in prods:
            self._need(o, pr)
        o.idx = len(self.q[eng])
        self.q[eng].append(o)
        return o

    def assign(self):
        for e in ENGS:
            c = 0
            for o in self.q[e]:
                if o.dma_sem is None and o.signal:
                    c += 1
                    o.semval = c

    def emit_engine(self, e, h, sems, dsems):
        for o in self.q[e]:
            for w in o.waits:
                if w.dma_sem is not None:
                    h.wait_ge(dsems[w.dma_sem], w.dma_val)
                else:
                    h.wait_ge(sems[w.eng], w.semval)
            inst = o.fn(h)
            if inst is None:
                continue
            if o.dma_sem is not None:
                inst.then_inc(dsems[o.dma_sem], 16)
            elif o.signal:
                inst.then_inc(sems[e], 1)


def _const_tables():
    slopes = np.exp2(-8.0 * np.arange(1, 9, dtype=np.float64) / 8.0)
    a = np.arange(128)[:, None]
    j = np.arange(640)[None, :]
    dl = a - j + 256
    ad = np.abs(dl)
    mult = (ad <= 64).astype(np.float64) + ((dl % 4 == 0) & (ad <= 256)).astype(np.float64)
    TA = np.empty((8, 128, 640), np.float32)
    for h in range(8):
        with np.errstate(divide="ignore"):
            bb = np.where(mult > 0, np.log(np.maximum(mult, 1e-30)) - slopes[h] * ad, NEG)
        TA[h] = bb.astype(np.float32)
    du = np.abs(np.arange(128)[:, None] - np.arange(128)[None, :])
    B3 = np.empty((8, 128, 128), np.float32)
    for h in range(8):
        B3[h] = np.where(du <= 64, -slopes[h] * 16.0 * du, NEG).astype(np.float32)
    half = 32
    pos = np.arange(T, dtype=np.float32)
    inv = np.power(np.float32(10000.0), -np.arange(half, dtype=np.float32) / half).astype(np.float32)
    ang = (pos[:, None] * inv[None, :]).astype(np.float32)
    cos = np.cos(ang).astype(np.float32).T
    sin = np.sin(ang).astype(np.float32).T
    cosT = np.concatenate([cos, cos], 0)
    sinT = np.concatenate([-sin, sin], 0)
    cosT, sinT = (np.ascontiguousarray(np.concatenate([cosT, cosT], 0)),
                  np.ascontiguousarray(np.concatenate([sinT, sinT], 0)))
    ident = np.eye(128, dtype=np.float32)
    return TA, B3, cosT, sinT, ident


def _prep_shared(w_ada, b_ada, g_pre, w_in, g_q_lora, w_uq, g_kv_lora, w_ukv, w_out, g_post):
    f = np.float32
    w_ada = np.asarray(w_ada, f)[0]
    b_ada = np.asarray(b_ada, f)[0]
    g_pre = np.asarray(g_pre, f)[0]
    w_in = np.asarray(w_in, f)[0]
    g_q = np.asarray(g_q_lora, f)[0]
    w_uq = np.asarray(w_uq, f)[0]
    g_kv = np.asarray(g_kv_lora, f)[0]
    w_ukv = np.asarray(w_ukv, f)[0]
    w_out = np.asarray(w_out, f)[0]
    g_post = np.asarray(g_post, f)[0]
    sh = {}
    sh["w_ada_l"] = np.ascontiguousarray(w_ada.reshape(16, 128, 12, 512).transpose(2, 1, 0, 3)).reshape(12, 128, 8192)
    sh["b_ss"] = np.ascontiguousarray(b_ada[0:4096].reshape(32, 128).T)
    sh["b_gate"] = np.ascontiguousarray(b_ada[4096:6144].reshape(1, 2048))
    sh["g_preT"] = np.ascontiguousarray(g_pre.reshape(16, 128).T)
    perm = []
    for h in range(8):
        for s in range(4):
            perm += list(range(s * 1024 + h * 128, s * 1024 + (h + 1) * 128))
    kpe = list(range(4864, 4928))
    kpe_sw = list(range(4896, 4928)) + list(range(4864, 4896))
    perm += list(range(4096, 4608)) + list(range(4608, 4864)) + kpe + kpe + kpe_sw + kpe_sw
    perm += list(range(4928, 5952))
    perm = np.asarray(perm)
    assert perm.size == NCH * 128
    wp = w_in[:, perm]
    sh["w_in_l"] = np.ascontiguousarray(wp.reshape(16, 128, NCH, 128).transpose(2, 1, 0, 3)).reshape(NCH, 128, 2048)
    pq = []
    for h in range(8):
        b0 = h * 192
        pq += list(range(b0, b0 + 128)) + list(range(b0 + 128, b0 + 192))
        pq += list(range(b0 + 160, b0 + 192)) + list(range(b0 + 128, b0 + 160))
    wq = w_uq[:, np.asarray(pq)]
    sh["w_uq_l"] = np.ascontiguousarray(wq.reshape(4, 128, 8, 256).transpose(2, 1, 0, 3)).reshape(8, 128, 1024)
    sh["w_ukv_l"] = np.ascontiguousarray(w_ukv.reshape(2, 128, 8, 256).transpose(2, 1, 0, 3)).reshape(8, 128, 512)
    sh["w_out_l"] = np.ascontiguousarray(w_out.reshape(16, 128, 2048).transpose(1, 0, 2)).reshape(128, 16 * 2048)
    sh["g_qT"] = np.ascontiguousarray(g_q.reshape(4, 128).T)
    sh["g_kvT"] = np.ascontiguousarray(g_kv.reshape(2, 128).T)
    sh["g_post"] = np.ascontiguousarray(g_post.reshape(1, 2048))
    TA, B3, cosT, sinT, ident = _const_tables()
    sh["TA"] = TA
    sh["B3"] = B3
    sh["cosT"] = cosT
    sh["sinT"] = sinT
    sh["csqT"] = np.ascontiguousarray(np.concatenate([cosT[0:64], sinT[0:64]], 0))
    sh["ident"] = ident
    return sh


_RC = {0: (0, 128), 1: (0, 256), 2: (0, 384), 3: (0, 512), 4: (0, 512), 5: (128, 512), 6: (256, 512), 7: (384, 512)}


def _run_merged(gp, np_, gs, ns):
    done_s = 0
    for i in range(np_):
        next(gp, None)
        target = ((i + 1) * ns) // np_ if np_ else ns
        while done_s < target:
            next(gs, None)
            done_s += 1
    for _ in gp:
        pass
    for _ in gs:
        pass


def build_nc(debug=False):
    nc = bass.Bass("TRN2", target_bir_lowering=False)

    def din(name, shape):
        return nc.dram_tensor(name, list(shape), F32, kind="ExternalInput").ap()

    x_d = din("x", [T, D])
    cT_d = din("cT", [128, 16])
    w_ada_d = din("w_ada_l", [12, 128, 16 * 512])
    b_ss_d = din("b_ss", [128, 32])
    b_gate_d = din("b_gate", [1, 2048])
    g_preT_d = din("g_preT", [128, 16])
    w_in_d = din("w_in_l", [NCH, 128, 16 * 128])
    w_uq_d = din("w_uq_l", [8, 128, 4 * 256])
    w_ukv_d = din("w_ukv_l", [8, 128, 2 * 256])
    w_out_d = din("w_out_l", [128, 16 * 2048])
    g_qT_d = din("g_qT", [128, 4])
    g_kvT_d = din("g_kvT", [128, 2])
    g_post_d = din("g_post", [1, 2048])
    TA_d = din("TA", [8, 128, 640])
    B3_d = din("B3", [8, 128, 128])
    cos_d = din("cosT", [128, T])
    sin_d = din("sinT", [128, T])
    csq_d = din("csqT", [128, T])
    ident_d = din("ident", [128, 128])
    out_d = nc.dram_tensor("out", [T, D], F32, kind="ExternalOutput").ap()
    dbg = {}
    if debug:
        dbg["hT"] = nc.dram_tensor("dbg_hT", [128, 16 * T], BF16, kind="ExternalOutput").ap()
        dbg["cqn"] = nc.dram_tensor("dbg_cqn", [128, 4 * T], BF16, kind="ExternalOutput").ap()
        dbg["ckvn"] = nc.dram_tensor("dbg_ckvn", [128, 2 * T], BF16, kind="ExternalOutput").ap()
        dbg["kpe"] = nc.dram_tensor("dbg_kpe", [64, T], BF16, kind="ExternalOutput").ap()
        dbg["yT"] = nc.dram_tensor("dbg_yT", [128, 16 * T], BF16, kind="ExternalOutput").ap()

    ARW = 52224
    with ExitStack() as es:
        arena = es.enter_context(nc.sbuf_tensor("arena", [128, ARW], F32))
        ps = [es.enter_context(nc.psum_tensor("ps%d" % i, [128, 512], F32)) for i in range(8)]
        A = arena[:]

        def v32(off, n):
            assert off % 4 == 0
            return A[:, off // 4: off // 4 + n]

        def v16(off, n):
            assert off % 4 == 0 and n % 2 == 0
            return A[:, off // 4: off // 4 + n // 2].bitcast(BF16)

        O_HT = 0
        O_YT = 65536
        O_LAT = 131072
        O_CQN = O_LAT
        O_CKVN = O_LAT + 16384
        O_KPE = O_LAT + 24576
        O_TR = 159744
        HT = v16(O_HT, 16 * T).rearrange("p (k t) -> p k t", k=16)
        YT = v16(O_YT, 16 * T).rearrange("p (k t) -> p k t", k=16)
        CQN = v16(O_CQN, 4 * T).rearrange("p (k t) -> p k t", k=4)
        CKVN = v16(O_CKVN, 2 * T).rearrange("p (k t) -> p k t", k=2)
        KPE = v16(O_KPE, T)
        O_ID = O_TR + 44032
        IDENT_BF = v16(O_ID, 128)
        ONES_BF = v16(O_ID + 256, 128)
        O_SM = O_TR + 44544
        SM = v32(O_SM, 256)
        c_sb = SM[:, 0:16]
        sc_f = SM[:, 16:32]
        MS = SM[:, 32:64]
        A_sc = SM[:, 64:80]
        g_pre_sb = SM[:, 80:96]
        b_ss_sb = SM[:, 96:128]
        g_q_sb = SM[:, 128:132]
        g_kv_sb = SM[:, 132:134]
        epsb = SM[:, 134:135]
        ssx = SM[:, 136:152]
        rstdx = SM[:, 152:168]
        rtmp = SM[:, 168:184]
        ssy = SM[:, 184:192]
        ssy1 = SM[:, 192:194]
        rty = SM[:, 194:196]
        rstdy = SM[:, 196:198]
        sc_bf = v16(O_SM + 200 * 4, 16)
        gateT = SM[:, 208:224]
        O_ID32 = O_TR + 45568
        IDENT_F = v32(O_ID32, 128)
        ONES_F = v32(O_ID32 + 512, 128)
        R2 = v32(O_TR + 46592, 512)
        B_sh = MS[:, 0:16]

        p = Prog()
        fin = []

        def ytk(off, n):
            return [("YT", i) for i in range(off // 4096, (off + n - 1) // 4096 + 1)]

        def latk(off, n):
            return [("LAT", i) for i in range(off // 4096, (off + n - 1) // 4096 + 1)]

        def trk(off, n):
            return [("TR", i) for i in range(off // 1024, (off + n - 1) // 1024 + 1)]

        p.op("pool", lambda e: e.dma_start(out=IDENT_BF, in_=ident_d), writes=["identbf"], dma_sem="c0")
        p.op("sp", lambda e: e.dma_start(out=IDENT_F, in_=ident_d), writes=["identf"], dma_sem="c1")
        p.op("sp", lambda e: e.dma_start(out=c_sb, in_=cT_d), writes=["c"], dma_sem="c2")
        p.op("sp", lambda e: e.dma_start(out=g_pre_sb, in_=g_preT_d), writes=["gpre"], dma_sem="c3")
        p.op("sp", lambda e: e.dma_start(out=b_ss_sb, in_=b_ss_d), writes=["bss"], dma_sem="c4")
        p.op("sp", lambda e: e.dma_start(out=g_q_sb, in_=g_qT_d), writes=["gq"], dma_sem="c5")
        p.op("sp", lambda e: e.dma_start(out=g_kv_sb, in_=g_kvT_d), writes=["gkv"], dma_sem="c6")
        p.op("dve", lambda e: e.memset(ONES_BF, 1.0), writes=["onesbf"])
        p.op("dve", lambda e: e.memset(ONES_F, 1.0), writes=["onesf"])
        p.op("dve", lambda e: e.memset(epsb, EPS), writes=["eps"])
        one_col = ONES_F[:, 0:1]
        p.op("act", lambda e: e.activation(out=sc_f, in_=c_sb, func=AF.Exp, scale=-1.0), reads=["c"], writes=["scf"])
        p.op("act", lambda e: e.activation(out=sc_f, in_=sc_f, func=AF.Ln, bias=one_col), reads=["scf", "onesf"], writes=["scf"])
        p.op("act", lambda e: e.activation(out=sc_f, in_=sc_f, func=AF.Exp, scale=-1.0), reads=["scf"], writes=["scf"])
        p.op("dve", lambda e: e.tensor_tensor(out=sc_f, in0=sc_f, in1=c_sb, op=ALU.mult), reads=["scf", "c"], writes=["scf"])
        p.op("dve", lambda e: e.tensor_copy(out=sc_bf, in_=sc_f), reads=["scf"], writes=["scbf"])

        WA = [v16(O_YT + s * 16384, 16 * 512).rearrange("p (k n) -> p k n", k=16) for s in range(2)]
        WAK = [ytk(s * 16384, 16384) for s in range(2)]
        NXT = 4
        XT = [v32(O_YT + 32768 + s * 8192, 2048) for s in range(NXT)]
        XTK = [ytk(32768 + s * 8192, 8192) for s in range(NXT)]
        JUNK = v16(O_LAT + 16384, 2048)
        JUNKK = latk(16384, 4096)
        DG = [v32(O_LAT + 20480 + s * 512, 128) for s in range(2)]

        def x_front(tb):
            s = tb % NXT
            s2 = tb % 2
            p.op("sp", lambda e: e.dma_start(out=XT[s], in_=x_d[tb * 128:(tb + 1) * 128, :]),
                 writes=XTK[s], dma_sem="xt%d" % s)
            p.op("act", lambda e: e.activation(out=JUNK, in_=XT[s], func=AF.Square, accum_out=ssx[:, tb:tb + 1]),
                 reads=XTK[s], writes=JUNKK + [("ssx", tb)])
            p.op("act", lambda e: e.activation(out=rtmp[:, tb:tb + 1], in_=ssx[:, tb:tb + 1], func=AF.Ln,
                                               bias=epsb, scale=1.0 / D),
                 reads=[("ssx", tb), "eps"], writes=[("rtmp", tb)])
            p.op("act", lambda e: e.activation(out=rstdx[:, tb:tb + 1], in_=rtmp[:, tb:tb + 1], func=AF.Exp, scale=-0.5),
                 reads=[("rtmp", tb)], writes=[("rstdx", tb)])
            p.op("dve", lambda e: e.tensor_scalar(out=DG[s2], in0=IDENT_F, scalar1=rstdx[:, tb:tb + 1], scalar2=1.0,
                                                  op0=ALU.mult, op1=ALU.mult),
                 reads=["identf", ("rstdx", tb)], writes=[("DG", s2)])
            for kc in range(16):
                b = 4 * s2 + kc // 4
                p.op("pe", lambda e, kc=kc, b=b: e.matmul(
                    ps[b][:, (kc % 4) * 128:(kc % 4 + 1) * 128], lhsT=XT[s][:, kc * 128:(kc + 1) * 128], rhs=DG[s2],
                    start=True, stop=True),
                    reads=XTK[s] + [("DG", s2)], writes=[("ps", b)])

        def x_back(tb):
            s = tb % 2
            for kc in range(16):
                b = 4 * s + kc // 4
                src = ps[b][:, (kc % 4) * 128:(kc % 4 + 1) * 128]
                dst = HT[:, kc, tb * 128:(tb + 1) * 128]
                if b % 4 != 3:
                    p.op("dve", lambda e, src=src, dst=dst, kc=kc: e.scalar_tensor_tensor(
                        out=dst, in0=src, scalar=A_sc[:, kc:kc + 1], in1=B_sh[:, kc:kc + 1].to_broadcast([128, 128]),
                        op0=ALU.mult, op1=ALU.add),
                        reads=[("ps", b), "Asc", "MS"], writes=[("HT", tb // 4, kc)])
                else:
                    p.op("act", lambda e, src=src, dst=dst, kc=kc: e.activation(
                        out=dst, in_=src, func=AF.Identity, bias=B_sh[:, kc:kc + 1], scale=A_sc[:, kc:kc + 1]),
                        reads=[("ps", b), "Asc", "MS"], writes=[("HT", tb // 4, kc)])

        psM = ps[7]
        for g in range(8):
            s = g % 2
            p.op("pool", lambda e, g=g, s=s: e.dma_start(out=WA[s].rearrange("p k n -> p (k n)"), in_=w_ada_d[g]),
                 writes=WAK[s], dma_sem="wa%d" % s)
            for sub in range(4):
                jn = 4 * g + sub
                for kc in range(16):
                    p.op("pe", lambda e, s=s, sub=sub, jn=jn, kc=kc: e.matmul(
                        psM[:, jn:jn + 1], lhsT=WA[s][:, kc, sub * 128:(sub + 1) * 128], rhs=sc_bf[:, kc:kc + 1],
                        start=(kc == 0), stop=(kc == 15)),
                        reads=WAK[s] + ["scbf"], writes=[("ps", 7)])
            if g == 5:
                x_front(0)
        p.op("dve", lambda e: e.tensor_tensor(out=MS, in0=psM[:, 0:32], in1=b_ss_sb, op=ALU.add),
             reads=[("ps", 7), "bss"], writes=["MS"])
        p.op("dve", lambda e: e.scalar_tensor_tensor(out=A_sc, in0=MS[:, 16:32], scalar=1.0, in1=g_pre_sb,
                                                      op0=ALU.add, op1=ALU.mult),
             reads=["MS", "gpre"], writes=["Asc"])
        for tb in range(16):
            if tb + 1 < 16:
                x_front(tb + 1)
            x_back(tb)
        HTK = [("HT", tc, kc) for tc in range(4) for kc in range(16)]

        NWIN = 2
        WIN = [v16(O_TR + i * 4096, 16 * 128).rearrange("p (k n) -> p k n", k=16) for i in range(NWIN)]
        WINK = [trk(i * 4096, 4096) for i in range(NWIN)]
        FA = v32(O_TR + 8192, 512)
        FB = v32(O_TR + 10240, 512)
        TA_sb = [v32(O_TR + 12288, 640)] * 2
        TAK = [["TA0"]] * 2
        B3_sb = [v32(O_TR + 14848, 128)] * 2
        B3K = [["B30"]] * 2
        TZ = v32(O_TR + 15360, 512)
        NPT = 5
        PT = [v16(O_TR + 18432 + i * 1024, 512) for i in range(NPT)]
        PTK = [["PT%d" % i] for i in range(NPT)]
        NTMP = 2
        TMP = [v32(O_TR + 23552 + i * 2048, 512) for i in range(NTMP)]
        TMPK = [["TMP%d" % i] for i in range(NTMP)]
        WG = v16(O_YT + 57344, 16 * 256).rearrange("p (k n) -> p k n", k=16)
        WGK = [("YT", 14), ("YT", 15)]
        U3S = v32(O_TR + 27648, 2048)
        L3S = v32(O_TR + 35840, 2048)
        U3SK = trk(27648, 8192)
        L3SK = trk(35840, 8192)

        def mkset(base, keyfn):
            d = {}
            names = ["QT", "KT", "VT", "VN", "V16", "GA"]
            for i, nm in enumerate(names):
                d[nm] = v16(base + i * 4096, T)
                d[nm + "K"] = keyfn(i)
            d["P3T"] = d["VT"].rearrange("p (r u) -> p r u", r=16)
            d["VN"] = d["VN"].rearrange("p (b d) -> p b d", b=16)
            d["V16"] = d["V16"].rearrange("p (r d) -> p r d", r=16)
            return d
        SETS = [mkset(O_LAT, lambda i: [("LAT", i)]), mkset(O_YT + 32768, lambda i: [("YT", 8 + i)])]

        tile_ctr = [0]
        ORDER = list(range(32)) + list(range(40, 48))
        pos_of = {j: i for i, j in enumerate(ORDER)}
        loaded = [0]

        def ensure_loaded(upto):
            while loaded[0] <= upto and loaded[0] < len(ORDER):
                pos = loaded[0]
                j = ORDER[pos]
                s = pos % NWIN
                p.op("pool", lambda e, s=s, j=j: e.dma_start(out=WIN[s].rearrange("p k n -> p (k n)"), in_=w_in_d[j]),
                     writes=WINK[s], dma_sem="win%d" % s)
                loaded[0] += 1

        nbanks = [2]

        def proj_tile(wt, wk, ncols, rhs_fn, nk, rkeys, m0=0, m1=128):
            b = tile_ctr[0] % nbanks[0]
            tile_ctr[0] += 1
            for kc in range(nk):
                p.op("pe", lambda e, kc=kc, b=b: e.matmul(
                    ps[b][0:(m1 - m0), 0:ncols], lhsT=wt[:, kc, m0:m1], rhs=rhs_fn(kc),
                    start=(kc == 0), stop=(kc == nk - 1)),
                    reads=wk + (rkeys(kc) if callable(rkeys) else rkeys), writes=[("ps", b)])
            return b

        win_slot = {}

        def ht_tile(j, tc, m0=0, m1=128):
            s = win_slot[j] if j in win_slot else pos_of[j] % NWIN
            return proj_tile(WIN[s], WINK[s], 512, lambda kc: HT[:, kc, tc * 512:(tc + 1) * 512], 16,
                             lambda kc: [("HT", tc, kc)], m0, m1)

        def silu_evac(b, dst, dstk):
            p.op("act", lambda e: e.activation(out=TZ, in_=ps[b][:, :], func=AF.Exp, scale=-1.0),
                 reads=[("ps", b)], writes=["TZ"])
            p.op("act", lambda e: e.activation(out=TZ, in_=TZ, func=AF.Ln, bias=one_col), reads=["TZ", "onesf"], writes=["TZ"])
            p.op("act", lambda e: e.activation(out=TZ, in_=TZ, func=AF.Exp, scale=-1.0), reads=["TZ"], writes=["TZ"])
            p.op("dve", lambda e: e.tensor_tensor(out=dst, in0=ps[b][:, :], in1=TZ, op=ALU.mult),
                 reads=[("ps", b), "TZ"], writes=dstk)

        def gen_inproj_head(h):
            S = SETS[(h + 1) % 2]
            hs = h % 2
            for sidx in range(4):
                j = 4 * h + sidx
                ensure_loaded(pos_of[j] + 1)
                for tc in range(4):
                    sl = slice(tc * 512, (tc + 1) * 512)
                    b = ht_tile(j, tc)
                    if sidx == 0:
                        p.op("dve", lambda e, b=b, sl=sl: e.tensor_copy(out=S["QT"][:, sl], in_=ps[b][:, :]),
                             reads=[("ps", b)], writes=S["QTK"])
                    elif sidx == 1:
                        p.op("act", lambda e, b=b, sl=sl: e.activation(out=S["KT"][:, sl], in_=ps[b][:, :], func=AF.Copy),
                             reads=[("ps", b)], writes=S["KTK"])
                    elif sidx == 2:
                        p.op("dve", lambda e, b=b, sl=sl: e.tensor_copy(out=S["VT"][:, sl], in_=ps[b][:, :]),
                             reads=[("ps", b)], writes=S["VTK"])
                    else:
                        silu_evac(b, S["GA"][:, sl], S["GAK"])
                    yield

        def gen_gate():
            for g8 in range(8):
                src = w_ada_d[8 + g8 // 2].rearrange("p (k n) -> p k n", k=16)[:, :, (g8 % 2) * 256:(g8 % 2 + 1) * 256]
                p.op("pool", lambda e, src=src: e.dma_start(out=WG, in_=src), writes=WGK, dma_sem="wg")
                b = tile_ctr[0] % 2
                tile_ctr[0] += 1
                for sub in range(2):
                    for kc in range(16):
                        p.op("pe", lambda e, sub=sub, kc=kc, b=b: e.matmul(
                            ps[b][:, sub:sub + 1], lhsT=WG[:, kc, sub * 128:(sub + 1) * 128], rhs=sc_bf[:, kc:kc + 1],
                            start=(kc == 0), stop=(kc == 15)),
                            reads=WGK + ["scbf"], writes=[("ps", b)])
                p.op("dve", lambda e, b=b, g8=g8: e.tensor_copy(out=gateT[:, 2 * g8:2 * g8 + 2], in_=ps[b][:, 0:2]),
                     reads=[("ps", b)], writes=[("gateT", g8)])
                yield

        def gen_attn(h):
            S = SETS[(h + 1) % 2]
            hs = h % 2
            p.op("sp", lambda e: e.dma_start(out=TA_sb[hs], in_=TA_d[h]), writes=TAK[hs], dma_sem="ta0")
            p.op("sp", lambda e: e.dma_start(out=B3_sb[hs], in_=B3_d[h]), writes=B3K[hs], dma_sem="b30")
            QT, KT, VT, P3T, VN, V16, GA = S["QT"], S["KT"], S["VT"], S["P3T"], S["VN"], S["V16"], S["GA"]
            QTK, KTK, VTK, VNK, V16K, GAK = S["QTK"], S["KTK"], S["VTK"], S["VNK"], S["V16K"], S["GAK"]
            cnt = [0]
            for which in range(2):
                for g in range(4):
                    sb = 2 + cnt[0] % 6
                    cnt[0] += 1
                    for bi in range(4):
                        idx = 4 * g + bi
                        src = VT[:, idx * 128:(idx + 1) * 128] if which == 0 else VT[:, idx:T:16]
                        p.op("pe", lambda e, sb=sb, bi=bi, src=src: e.matmul(ps[sb][:, bi * 128:(bi + 1) * 128], lhsT=src,
                                                                          rhs=IDENT_BF, start=True, stop=True),
                             reads=VTK + ["identbf"], writes=[("ps", sb)])
                    dst = (VN if which == 0 else V16)[:, 4 * g:4 * g + 4, :]
                    dk = VNK if which == 0 else V16K
                    srcp = ps[sb][:, :].rearrange("p (b d) -> p b d", b=4)
                    if g % 2 == 0:
                        p.op("dve", lambda e, dst=dst, srcp=srcp: e.tensor_copy(out=dst, in_=srcp),
                             reads=[("ps", sb)], writes=dk)
                    else:
                        p.op("act", lambda e, dst=dst, srcp=srcp: e.activation(out=dst, in_=srcp, func=AF.Copy),
                             reads=[("ps", sb)], writes=dk)
                    yield
            for g in range(4):
                sb = 2 + cnt[0] % 6
                ti = cnt[0] % NTMP
                cnt[0] += 1
                for ri in range(4):
                    r = 4 * g + ri
                    p.op("pe", lambda e, sb=sb, ri=ri, r=r: e.matmul(ps[sb][:, ri * 128:(ri + 1) * 128],
                                                                    lhsT=KT[:, r:T:16], rhs=QT[:, r:T:16],
                                                                    start=True, stop=True),
                         reads=KTK + QTK, writes=[("ps", sb)])
                p.op("dve", lambda e, sb=sb, ti=ti: e.scalar_tensor_tensor(
                    out=TMP[ti].rearrange("p (r u) -> p r u", r=4), in0=ps[sb][:, :].rearrange("p (r u) -> p r u", r=4),
                    scalar=SC_A, in1=B3_sb[hs].unsqueeze(1).to_broadcast([128, 4, 128]), op0=ALU.mult, op1=ALU.add),
                    reads=[("ps", sb)] + B3K[hs], writes=TMPK[ti])
                p.op("act", lambda e, g=g, ti=ti: e.activation(out=P3T[:, 4 * g:4 * g + 4, :],
                                                              in_=TMP[ti].rearrange("p (r u) -> p r u", r=4), func=AF.Exp),
                     reads=TMPK[ti], writes=VTK)
                yield
            U3v = U3S.rearrange("p (u r) -> p r u", r=16)
            L3v = L3S.rearrange("p (u r) -> p r u", r=16)
            for g in range(4):
                sbu = 2 + cnt[0] % 6
                sbl = 2 + (cnt[0] + 1) % 6
                cnt[0] += 2
                for ri in range(4):
                    r = 4 * g + ri
                    p.op("pe", lambda e, sbu=sbu, ri=ri, r=r: e.matmul(ps[sbu][:, ri * 128:(ri + 1) * 128],
                                                                      lhsT=V16[:, r, :], rhs=P3T[:, r, :],
                                                                      start=True, stop=True),
                         reads=V16K + VTK, writes=[("ps", sbu)])
                for ri in range(4):
                    r = 4 * g + ri
                    p.op("pe", lambda e, sbl=sbl, ri=ri, r=r: e.matmul(ps[sbl][:, ri * 128:(ri + 1) * 128],
                                                                      lhsT=ONES_BF, rhs=P3T[:, r, :],
                                                                      start=True, stop=True),
                         reads=["onesbf"] + VTK, writes=[("ps", sbl)])
                p.op("dve", lambda e, sbu=sbu, g=g: e.tensor_copy(
                    out=U3v[:, 4 * g:4 * g + 4, :], in_=ps[sbu][:, :].rearrange("p (r u) -> p r u", r=4)),
                    reads=[("ps", sbu)], writes=U3SK)
                p.op("act", lambda e, sbl=sbl, g=g: e.activation(
                    out=L3v[:, 4 * g:4 * g + 4, :], in_=ps[sbl][:, :].rearrange("p (r u) -> p r u", r=4), func=AF.Copy),
                    reads=[("ps", sbl)], writes=L3SK)
                yield
            defer = []

            def flush(keep=0):
                while len(defer) > keep:
                    defer.pop(0)()

            for qc in range(4):
                order = [c for c in (3, 4, 2, 5, 1, 6, 0, 7) if 0 <= 4 * qc - 2 + c <= 15]
                for si, c in enumerate(order):
                    kb = 4 * qc - 2 + c
                    i0, i1 = _RC[c]
                    n = i1 - i0
                    j0 = i0 + 512 - 128 * c
                    sb = 2 + cnt[0] % 4
                    pi = cnt[0] % NPT
                    ti = cnt[0] % NTMP
                    cnt[0] += 1
                    p.op("pe", lambda e, sb=sb, kb=kb, i0=i0, i1=i1, qc=qc: e.matmul(
                        ps[sb][:, i0:i1], lhsT=KT[:, kb * 128:(kb + 1) * 128], rhs=QT[:, qc * 512 + i0:qc * 512 + i1],
                        start=True, stop=True),
                        reads=KTK + QTK, writes=[("ps", sb)])
                    flush(3)
                    p.op("dve", lambda e, sb=sb, ti=ti, i0=i0, i1=i1, j0=j0, n=n: e.scalar_tensor_tensor(
                        out=TMP[ti][:, i0:i1], in0=ps[sb][:, i0:i1], scalar=SC_A, in1=TA_sb[hs][:, j0:j0 + n],
                        op0=ALU.mult, op1=ALU.add),
                        reads=[("ps", sb)] + TAK[hs], writes=TMPK[ti])
                    p.op("act", lambda e, pi=pi, ti=ti, i0=i0, i1=i1: e.activation(out=PT[pi][:, i0:i1],
                                                                                  in_=TMP[ti][:, i0:i1], func=AF.Exp),
                         reads=TMPK[ti], writes=PTK[pi])

                    def pv(kb=kb, pi=pi, i0=i0, i1=i1, first=(si == 0), last=(si == len(order) - 1), bu=6):
                        p.op("pe", lambda e: e.matmul(ps[bu][:, i0:i1], lhsT=VN[:, kb, :], rhs=PT[pi][:, i0:i1],
                                                      start=first, stop=last, skip_group_check=True),
                             reads=VNK + PTK[pi], writes=[("ps", bu)])
                        p.op("pe", lambda e: e.matmul(ps[bu + 1][:, i0:i1], lhsT=ONES_BF, rhs=PT[pi][:, i0:i1],
                                                      start=first, stop=last, skip_group_check=True),
                             reads=["onesbf"] + PTK[pi], writes=[("ps", bu + 1)])
                    defer.append(pv)
                    yield
                flush(0)
                qs = slice(qc * 512, (qc + 1) * 512)
                bu = 6
                p.op("dve", lambda e, qs=qs, bu=bu: e.tensor_tensor(out=FA, in0=ps[bu + 1][:, :], in1=L3S[:, qs], op=ALU.add),
                     reads=[("ps", bu + 1)] + L3SK, writes=["FA"])
                p.op("dve", lambda e, qs=qs, bu=bu: e.tensor_tensor(out=FB, in0=ps[bu][:, :], in1=U3S[:, qs], op=ALU.add),
                     reads=[("ps", bu)] + U3SK, writes=["FB"])
                p.op("act", lambda e: e.activation(out=FA, in_=FA, func=AF.Ln), reads=["FA"], writes=["FA"])
                p.op("act", lambda e: e.activation(out=FA, in_=FA, func=AF.Exp, scale=-1.0), reads=["FA"], writes=["FA"])
                p.op("pool", lambda e: e.tensor_tensor(out=FB, in0=FB, in1=FA, op=ALU.mult),
                     reads=["FA", "FB"], writes=["FB"])
                p.op("pool", lambda e, qs=qs: e.tensor_tensor(out=YT[:, h, qs], in0=FB, in1=GA[:, qs], op=ALU.mult),
                     reads=["FB"] + GAK, writes=[("YT", h)])
                yield

        N_ATT = 8 + 4 + 4 + 4 * 1 + sum(len([c for c in range(8) if 0 <= 4 * qc - 2 + c <= 15]) for qc in range(4))

        def gen_bz():
            for h in range(8):
                j = 40 + h
                ensure_loaded(pos_of[j] + 1)
                for tc in range(4):
                    b = ht_tile(j, tc)
                    silu_evac(b, YT[:, 8 + h, tc * 512:(tc + 1) * 512], [("YT", 8 + h)])
                    yield

        ensure_loaded(1)
        for _ in gen_inproj_head(0):
            pass
        gate_gen = gen_gate()

        def attn_with_gate(h, every):
            for i, _ in enumerate(gen_attn(h)):
                if i % every == every - 1:
                    next(gate_gen, None)
                yield

        for h in range(8):
            if h < 7:
                sec = attn_with_gate(h, N_ATT // 4) if h in (1, 2) else gen_attn(h)
                _run_merged(gen_inproj_head(h + 1), 16, sec, N_ATT)
            else:
                _run_merged(gen_bz(), 32, gen_attn(h), N_ATT)
        for _ in gate_gen:
            pass

        p.barrier()
        O_LT = O_TR + 8192
        CQRAW = [v32(O_LT + j * 8192, 2048) for j in range(4)]
        defer = []

        def flush():
            while defer:
                defer.pop(0)()

        def evac_latent(tc, b, raw, rawk, SQl, SQK, first, last, ctr):
            sl = slice(tc * 512, (tc + 1) * 512)
            p.op("act", lambda e: e.activation(out=raw[:, sl], in_=ps[b][:, :], func=AF.Copy),
                 reads=[("ps", b)], writes=rawk)
            i = ctr[0] % 2
            ctr[0] += 1
            p.op("act", lambda e: e.activation(out=SQl[i], in_=ps[b][:, :], func=AF.Square),
                 reads=[("ps", b)], writes=SQK[i])
            defer.append(lambda: p.op("pe", lambda e: e.matmul(
                ps[2 + tc][:, :], lhsT=ONES_F, rhs=SQl[i], start=first, stop=last),
                reads=SQK[i] + ["onesf"], writes=[("ps", 2 + tc)]))

        def norm_latent(raws, rawks, nch, gsb, dst, dstk, nfeat, RSl, RSK):
            for tc in range(4):
                i = tc % 2
                p.op("act", lambda e, tc=tc, i=i: e.activation(out=RSl[i], in_=ps[2 + tc][:, :], func=AF.Ln,
                                                               bias=epsb, scale=1.0 / nfeat),
                     reads=[("ps", 2 + tc), "eps"], writes=RSK[i])
                p.op("act", lambda e, i=i: e.activation(out=RSl[i], in_=RSl[i], func=AF.Exp, scale=-0.5),
                     reads=RSK[i], writes=RSK[i])
                for jj in range(nch):
                    p.op("dve", lambda e, jj=jj, tc=tc, i=i: e.scalar_tensor_tensor(
                        out=dst[:, jj, tc * 512:(tc + 1) * 512], in0=raws[jj][:, tc * 512:(tc + 1) * 512],
                        scalar=gsb[:, jj:jj + 1], in1=RSl[i], op0=ALU.mult, op1=ALU.mult),
                        reads=rawks[jj] + RSK[i] + ["gq", "gkv"], writes=dstk(jj))

        lat_loaded = [32]

        def load_lat(j):
            while lat_loaded[0] <= j and lat_loaded[0] <= 39:
                jj_ = lat_loaded[0]
                s = jj_ % 2
                win_slot[jj_] = s
                p.op("pool", lambda e, s=s, jj_=jj_: e.dma_start(out=WIN[s].rearrange("p k n -> p (k n)"), in_=w_in_d[jj_]),
                     writes=WINK[s], dma_sem="win%d" % s)
                lat_loaded[0] += 1

        load_lat(33)
        SQa = [v32(O_CKVN + i * 2048, 512) for i in range(2)]
        RSa = [v32(O_CKVN + 4096 + i * 2048, 512) for i in range(2)]
        SQaK = [["SQa%d" % i] for i in range(2)]
        RSaK = [["RSa%d" % i] for i in range(2)]
        CQRAWK = [["CQRAW%d" % j] for j in range(4)]
        ctr = [0]
        for jj in range(4):
            j = 32 + jj
            load_lat(j + 1)
            for tc in range(4):
                b = ht_tile(j, tc)
                flush()
                evac_latent(tc, b, CQRAW[jj], CQRAWK[jj], SQa, SQaK, jj == 0, jj == 3, ctr)
        flush()
        norm_latent(CQRAW, CQRAWK, 4, g_q_sb, CQN, lambda jj: [("LAT", jj)], 512, RSa, RSaK)
        p.barrier()
        CKRAW = [v32(O_LT + j * 8192, 2048) for j in range(2)]
        CKRAWK = [["CKRAW%d" % j] for j in range(2)]
        SQb = [v32(O_LT + 16384 + i * 2048, 512) for i in range(2)]
        RSb = [v32(O_LT + 20480 + i * 2048, 512) for i in range(2)]
        SQbK = [["SQb%d" % i] for i in range(2)]
        RSbK = [["RSb%d" % i] for i in range(2)]
        for jj in range(2):
            j = 36 + jj
            load_lat(j + 1)
            for tc in range(4):
                b = ht_tile(j, tc)
                flush()
                evac_latent(tc, b, CKRAW[jj], CKRAWK[jj], SQb, SQbK, jj == 0, jj == 1, ctr)
        flush()
        norm_latent(CKRAW, CKRAWK, 2, g_kv_sb, CKVN, lambda jj: [("LAT", 4 + jj)], 256, RSb, RSbK)
        p.barrier()
        COS1 = v32(O_LT, 2048)
        SIN1 = v32(O_LT + 8192, 2048)
        T1 = v32(O_LT + 16384, 512)
        T2 = v32(O_LT + 18432, 512)
        p.op("sp", lambda e: e.dma_start(out=COS1, in_=cos_d), writes=["COS1"], dma_sem="cos")
        p.op("sp", lambda e: e.dma_start(out=SIN1, in_=sin_d), writes=["SIN1"], dma_sem="sin")
        load_lat(39)
        for tc in range(4):
            sl = slice(tc * 512, (tc + 1) * 512)
            ba = ht_tile(38, tc)
            bb = ht_tile(39, tc)
            p.op("dve", lambda e, ba=ba, sl=sl: e.tensor_tensor(out=T1, in0=ps[ba][:, :], in1=COS1[:, sl], op=ALU.mult),
                 reads=[("ps", ba), "COS1"], writes=["T1"])
            p.op("dve", lambda e, bb=bb, sl=sl: e.tensor_tensor(out=T2, in0=ps[bb][:, :], in1=SIN1[:, sl], op=ALU.mult),
                 reads=[("ps", bb), "SIN1"], writes=["T2"])
            p.op("dve", lambda e, sl=sl: e.tensor_tensor(out=KPE[:, sl], in0=T1, in1=T2, op=ALU.add),
                 reads=["T1", "T2"], writes=[("LAT", 6)])

        if debug:
            fin.append(p.op("sp", lambda e: e.dma_start(out=dbg["hT"], in_=HT.rearrange("p k t -> p (k t)")),
                            reads=HTK, dma_sem="dbg0"))
            fin.append(p.op("sp", lambda e: e.dma_start(out=dbg["cqn"], in_=CQN.rearrange("p k t -> p (k t)")),
                            reads=[("LAT", i) for i in range(4)], dma_sem="dbg1"))
            fin.append(p.op("sp", lambda e: e.dma_start(out=dbg["ckvn"], in_=CKVN.rearrange("p k t -> p (k t)")),
                            reads=[("LAT", 4), ("LAT", 5)], dma_sem="dbg2"))
            fin.append(p.op("sp", lambda e: e.dma_start(out=dbg["kpe"], in_=KPE[0:64, :]), reads=[("LAT", 6)],
                            dma_sem="dbg3"))

        p.barrier()
        WOUT = v16(O_HT, 16 * 2048).rearrange("p (k n) -> p k n", k=16)
        WUQ = v16(O_TR + 0, 4 * 256).rearrange("p (k n) -> p k n", k=4)
        WUKV = v16(O_TR + 2048, 2 * 256).rearrange("p (k n) -> p k n", k=2)
        QN = v16(O_TR + 4096, T)
        KN = v16(O_TR + 8192, T)
        QPE = v16(O_TR + 12288, T)
        VB = v16(O_TR + 16384, T).rearrange("p (b d) -> p b d", b=16)
        PT2 = [v16(O_TR + 20480 + i * 1024, 512) for i in range(6)]
        R1 = v32(O_TR + 26624, 512)
        CSQ = v32(O_HT + 49152, 2048)
        CQNK = [("LAT", i) for i in range(4)]
        CKVNK = [("LAT", 4), ("LAT", 5)]
        KPEK = [("LAT", 6)]

        def load_wb(h):
            p.op("pool", lambda e: e.dma_start(out=WUQ.rearrange("p k n -> p (k n)"), in_=w_uq_d[h]),
                 writes=["WUQ"], dma_sem="wuq")
            p.op("pool", lambda e: e.dma_start(out=WUKV.rearrange("p k n -> p (k n)"), in_=w_ukv_d[h]),
                 writes=["WUKV"], dma_sem="wukv")

        load_wb(0)
        p.op("sp", lambda e: e.dma_start(out=CSQ, in_=csq_d), writes=[("WOUT", 3)], dma_sem="cos")

        def load_wout(g):
            p.op("pool", lambda e: e.dma_start(
                out=WOUT[:, 4 * g:4 * g + 4, :].rearrange("p k n -> p (k n)"),
                in_=w_out_d[:, g * 8192:(g + 1) * 8192]),
                writes=[("WOUT", g)], dma_sem="wout%d" % g)

        nbanks[0] = 4
        for h in range(8):
            for tc in range(4):
                sl = slice(tc * 512, (tc + 1) * 512)
                b = proj_tile(WUKV, ["WUKV"], 512, lambda kc, sl=sl: CKVN[:, kc, sl], 2, CKVNK, 0, 128)
                p.op("act", lambda e, b=b, sl=sl: e.activation(out=KN[:, sl], in_=ps[b][:, :], func=AF.Copy),
                     reads=[("ps", b)], writes=["KN"])
            for tc in range(4):
                sl = slice(tc * 512, (tc + 1) * 512)
                b = proj_tile(WUQ, ["WUQ"], 512, lambda kc, sl=sl: CQN[:, kc, sl], 4, CQNK, 0, 128)
                p.op("dve", lambda e, b=b, sl=sl: e.tensor_copy(out=QN[:, sl], in_=ps[b][:, :]),
                     reads=[("ps", b)], writes=["QN"])
                ba = proj_tile(WUQ, ["WUQ"], 512, lambda kc, sl=sl: CQN[:, kc, sl], 4, CQNK, 128, 256)
                p.op("dve", lambda e, ba=ba, sl=sl: e.tensor_tensor(out=QPE[:, sl], in0=ps[ba][:, :], in1=CSQ[:, sl],
                                                                    op=ALU.mult),
                     reads=[("ps", ba), ("WOUT", 3)], writes=["QPE"])
            for g in range(4):
                b = tile_ctr[0] % nbanks[0]
                tile_ctr[0] += 1
                for bi in range(4):
                    blk = 4 * g + bi
                    for kc in range(2):
                        p.op("pe", lambda e, b=b, bi=bi, blk=blk, kc=kc: e.matmul(
                            ps[b][:, bi * 128:(bi + 1) * 128], lhsT=CKVN[:, kc, blk * 128:(blk + 1) * 128],
                            rhs=WUKV[:, kc, 128:256], start=(kc == 0), stop=(kc == 1)),
                            reads=CKVNK + ["WUKV"], writes=[("ps", b)])
                p.op("dve", lambda e, b=b, g=g: e.tensor_copy(
                    out=VB[:, 4 * g:4 * g + 4, :], in_=ps[b][:, :].rearrange("p (b d) -> p b d", b=4)),
                    reads=[("ps", b)], writes=["VB"])
            if h < 7:
                load_wb(h + 1)
            if h < 3:
                load_wout(h)
            if h == 7:
                load_wout(3)
            defer = []

            def flush2(keep=0):
                while len(defer) > keep:
                    defer.pop(0)()

            cnt = 0
            for qc in range(4):
                qs = slice(qc * 512, (qc + 1) * 512)
                for kb in range(16):
                    sb = 2 + cnt % 2
                    pi = cnt % 6
                    cnt += 1
                    ks = slice(kb * 128, (kb + 1) * 128)
                    p.op("pe", lambda e, sb=sb, ks=ks, qs=qs: e.matmul(ps[sb][:, :], lhsT=KN[:, ks], rhs=QN[:, qs],
                                                                      start=True, stop=False),
                         reads=["KN", "QN"], writes=[("ps", sb)])
                    p.op("pe", lambda e, sb=sb, ks=ks, qs=qs: e.matmul(ps[sb][:, :], lhsT=KPE[:, ks], rhs=QPE[:, qs],
                                                                      start=False, stop=True),
                         reads=KPEK + ["QPE"], writes=[("ps", sb)])
                    flush2(2)
                    p.op("act", lambda e, sb=sb, pi=pi: e.activation(out=PT2[pi], in_=ps[sb][:, :], func=AF.Exp,
                                                                    scale=SC_B),
                         reads=[("ps", sb)], writes=[("PT2", pi)])

                    def pv(kb=kb, pi=pi, bu=4 + 2 * (qc % 2)):
                        p.op("pe", lambda e: e.matmul(ps[bu][:, :], lhsT=VB[:, kb, :], rhs=PT2[pi],
                                                      start=(kb == 0), stop=(kb == 15)),
                             reads=["VB", ("PT2", pi)], writes=[("ps", bu)])
                        p.op("pe", lambda e: e.matmul(ps[bu + 1][:, :], lhsT=ONES_BF, rhs=PT2[pi],
                                                      start=(kb == 0), stop=(kb == 15)),
                             reads=["onesbf", ("PT2", pi)], writes=[("ps", bu + 1)])
                    defer.append(pv)
                yv = YT[:, 8 + h, qs]
                bu = 4 + 2 * (qc % 2)

                def fin_qc(yv=yv, bu=bu):
                    p.op("act", lambda e: e.activation(out=R1, in_=ps[bu + 1][:, :], func=AF.Ln),
                         reads=[("ps", bu + 1)], writes=["R1"])
                    p.op("act", lambda e: e.activation(out=R1, in_=R1, func=AF.Exp, scale=-1.0), reads=["R1"], writes=["R1"])
                    p.op("dve", lambda e: e.tensor_tensor(out=R2, in0=ps[bu][:, :], in1=R1, op=ALU.mult),
                         reads=[("ps", bu), "R1"], writes=["R2"])
                    p.op("pool", lambda e: e.tensor_tensor(out=yv, in0=R2, in1=yv, op=ALU.mult),
                         reads=["R2", ("YT", 8 + h)], writes=[("YT", 8 + h)])
                defer.append(fin_qc)
            flush2(0)
            if False:
                p.op("dve", lambda e: e.reciprocal(out=R1, in_=ps[7][:, :]), reads=[("ps", 7)], writes=["R1"])
                p.op("dve", lambda e: e.tensor_tensor(out=R2, in0=ps[6][:, :], in1=R1, op=ALU.mult),
                     reads=[("ps", 6), "R1"], writes=["R2"])
                p.op("dve", lambda e, yv=yv: e.tensor_tensor(out=yv, in0=R2, in1=yv, op=ALU.mult),
                     reads=["R2", ("YT", 8 + h)], writes=[("YT", 8 + h)])

        if debug:
            fin.append(p.op("sp", lambda e: e.dma_start(out=dbg["yT"], in_=YT.rearrange("p k t -> p (k t)")),
                            reads=[("YT", i) for i in range(16)], dma_sem="dbg4"))

        p.barrier()
        RG = v32(O_LAT, 2048)
        GG = v32(O_LAT + 8192, 2048)
        BG = v32(O_LAT + 16384, 2048)
        GP = v32(O_TR + 0, 2048)
        p.op("sp", lambda e: e.dma_start(out=BG, in_=b_gate_d.partition_broadcast(128)), writes=["BG"], dma_sem="c2")
        p.op("sp", lambda e: e.dma_start(out=GP, in_=g_post_d.partition_broadcast(128)), writes=["GP"], dma_sem="c3")
        for kc in range(16):
            p.op("dve", lambda e, kc=kc: e.tensor_scalar(out=RG[:, kc * 128:(kc + 1) * 128], in0=IDENT_F,
                                                         scalar1=gateT[:, kc:kc + 1], scalar2=1.0,
                                                         op0=ALU.mult, op1=ALU.mult),
                 reads=["identf"] + [("gateT", kc // 2)], writes=[("RG", kc // 4)])
        for g in range(4):
            gs = slice(g * 512, (g + 1) * 512)
            p.op("pe", lambda e, g=g, gs=gs: e.matmul(ps[g][:, :], lhsT=ONES_F, rhs=RG[:, gs], start=True, stop=True),
                 reads=["onesf", ("RG", g)], writes=[("ps", g)])
            p.op("dve", lambda e, g=g, gs=gs: e.tensor_tensor(out=GG[:, gs], in0=ps[g][:, :], in1=BG[:, gs], op=ALU.add),
                 reads=[("ps", g), "BG"], writes=[("GG", g)])
            p.op("dve", lambda e, gs=gs: e.tensor_tensor(out=GG[:, gs], in0=GG[:, gs], in1=GP[:, gs], op=ALU.mult),
                 reads=[("GG", g), "GP"], writes=[("GG", g)])
        XR = [v32(O_TR + 8192 + s * 8192, 2048) for s in range(2)]
        RES = [v32(O_TR + 24576 + s * 8192, 2048) for s in range(2)]
        JK = v16(O_TR + 40960, 512)
        outs = []
        YTK = [("YT", i) for i in range(16)]
        for tb in range(16):
            s = tb % 2
            ts_ = slice(tb * 128, (tb + 1) * 128)
            p.op("sp", lambda e, s=s, ts_=ts_: e.dma_start(out=XR[s], in_=x_d[ts_, :]), writes=[("XR", s)],
                 dma_sem="xt%d" % s)
            for nb in range(4):
                b = 4 * s + nb
                for kc in range(16):
                    p.op("pe", lambda e, b=b, kc=kc, nb=nb, ts_=ts_: e.matmul(
                        ps[b][:, :], lhsT=YT[:, kc, ts_], rhs=WOUT[:, kc, nb * 512:(nb + 1) * 512],
                        start=(kc == 0), stop=(kc == 15)),
                        reads=YTK + [("WOUT", kc // 4)], writes=[("ps", b)])
            for nb in range(4):
                b = 4 * s + nb
                p.op("act", lambda e, b=b, s=s, nb=nb: e.activation(out=JK, in_=ps[b][:, :], func=AF.Square,
                                                                    accum_out=ssy[:, 4 * s + nb:4 * s + nb + 1]),
                     reads=[("ps", b)], writes=["JK", ("ssy", s, nb)])
            p.op("dve", lambda e, s=s: e.reduce_sum(out=ssy1[:, s:s + 1], in_=ssy[:, 4 * s:4 * s + 4], axis=AX.X),
                 reads=[("ssy", s, nb) for nb in range(4)], writes=[("ssy1", s)])
            p.op("act", lambda e, s=s: e.activation(out=rty[:, s:s + 1], in_=ssy1[:, s:s + 1], func=AF.Ln,
                                                    bias=epsb, scale=1.0 / D),
                 reads=[("ssy1", s), "eps"], writes=[("rty", s)])
            p.op("act", lambda e, s=s: e.activation(out=rstdy[:, s:s + 1], in_=rty[:, s:s + 1], func=AF.Exp, scale=-0.5),
                 reads=[("rty", s)], writes=[("rstdy", s)])
            for nb in range(4):
                b = 4 * s + nb
                ns = slice(nb * 512, (nb + 1) * 512)
                p.op("dve", lambda e, b=b, s=s, ns=ns: e.scalar_tensor_tensor(
                    out=RES[s][:, ns], in0=ps[b][:, :], scalar=rstdy[:, s:s + 1], in1=GG[:, ns],
                    op0=ALU.mult, op1=ALU.mult),
                    reads=[("ps", b), ("rstdy", s)] + [("GG", nb)], writes=[("RES", s, nb)])
            p.op("pool", lambda e, s=s: e.tensor_tensor(out=RES[s], in0=RES[s], in1=XR[s], op=ALU.add),
                 reads=[("RES", s, nb) for nb in range(4)] + [("XR", s)], writes=[("RES", s, nb) for nb in range(4)])
            outs.append(p.op("sp", lambda e, s=s, ts_=ts_: e.dma_start(out=out_d[ts_, :], in_=RES[s]),
                             reads=[("RES", s, nb) for nb in range(4)], dma_sem="out%d" % s))
        p.wait_only("sp", outs[-2:] + fin)
        p.assign()

        sems = {e: es.enter_context(nc.semaphore("s_" + e)) for e in ENGS}
        dsems = {n: es.enter_context(nc.semaphore("d_" + n)) for n in p.dma_names}
        block = es.enter_context(nc.Block())

        @block.tensor
        def _(e):
            p.emit_engine("pe", e, sems, dsems)

        @block.scalar
        def _(e):
            p.emit_engine("act", e, sems, dsems)

        @block.vector
        def _(e):
            p.emit_engine("dve", e, sems, dsems)

        @block.gpsimd
        def _(e):
            p.emit_engine("pool", e, sems, dsems)

        @block.sync
        def _(e):
            p.emit_engine("sp", e, sems, dsems)
    return nc


_NC_CACHE = {}


def kernel(x, c, w_ada, b_ada, g_pre, w_in, g_q_lora, w_uq, g_kv_lora, w_ukv, w_out, g_post):
    x = np.asarray(x, np.float32)
    c = np.asarray(c, np.float32)
    sh = _prep_shared(w_ada, b_ada, g_pre, w_in, g_q_lora, w_uq, g_kv_lora, w_ukv, w_out, g_post)
    if "nc" not in _NC_CACHE:
        _NC_CACHE["nc"] = build_nc()
    nc = _NC_CACHE["nc"]
    in_maps = []
    for b in range(8):
        m = dict(sh)
        m["x"] = np.ascontiguousarray(x[b])
        m["cT"] = np.ascontiguousarray(c[b].reshape(16, 128).T)
        in_maps.append(m)
    res = run_bass_kernel_spmd(nc, in_maps, core_ids=list(range(8)))
    return np.stack([np.asarray(r["out"], np.float32) for r in res.results], axis=0)
```

```python
import numpy as np
from contextlib import ExitStack
import concourse.bass as bass
import concourse.mybir as mybir
from concourse.bass_utils import run_bass_kernel_spmd

F32 = mybir.dt.float32
BF16 = mybir.dt.bfloat16
AF = mybir.ActivationFunctionType
ALU = mybir.AluOpType
AX = mybir.AxisListType

T = 2048
D = 2048
NCH = 48
EPS = 1e-6
SC_A = 128.0 ** -0.5
SC_B = 192.0 ** -0.5
NEG = -30000.0

ENGS = ("pe", "act", "dve", "pool", "sp")


class Op:
    __slots__ = ("eng", "fn", "waits", "signal", "semval", "idx", "dma_sem", "dma_val")

    def __init__(self, eng, fn):
        self.eng = eng
        self.fn = fn
        self.waits = []
        self.signal = False
        self.semval = None
        self.idx = None
        self.dma_sem = None
        self.dma_val = None


class Prog:
    def __init__(self):
        self.q = {e: [] for e in ENGS}
        self.res = {}
        self.waited = {e: {} for e in ENGS}
        self.dma_cnt = {}
        self.dma_names = []
        self.pending_barrier = {e: [] for e in ENGS}

    def _need(self, op, prod):
        if prod is None:
            return
        if prod.dma_sem is not None:
            key = ("dma", prod.dma_sem)
            val = prod.dma_val
        else:
            if prod.eng == op.eng and op.eng in ("pe", "sp"):
                return
            key = ("eng", prod.eng)
            val = prod.idx
        if self.waited[op.eng].get(key, -1) >= val:
            return
        self.waited[op.eng][key] = val
        prod.signal = True
        op.waits.append(prod)

    def barrier(self):
        lasts = []
        for e in ENGS:
            if e == "sp":
                continue
            for o in reversed(self.q[e]):
                if o.dma_sem is None:
                    lasts.append(o)
                    break
        dmas = {}
        for e in ENGS:
            for o in self.q[e]:
                if o.dma_sem is not None:
                    dmas[o.dma_sem] = o
        lasts += list(dmas.values())
        for e in ENGS:
            self.pending_barrier[e] = list(lasts)

    def op(self, eng, fn, reads=(), writes=(), dma_sem=None):
        o = Op(eng, fn)
        if self.pending_barrier[eng]:
            for pr in self.pending_barrier[eng]:
                self._need(o, pr)
            self.pending_barrier[eng] = []
        for k in reads:
            r = self.res.get(k)
            if r is not None:
                self._need(o, r[0])
        for k in writes:
            r = self.res.get(k)
            if r is not None:
                self._need(o, r[0])
                for rd in r[1]:
                    self._need(o, rd)
        o.idx = len(self.q[eng])
        self.q[eng].append(o)
        if dma_sem is not None:
            if dma_sem not in self.dma_cnt:
                self.dma_names.append(dma_sem)
            o.dma_sem = dma_sem
            self.dma_cnt[dma_sem] = self.dma_cnt.get(dma_sem, 0) + 16
            o.dma_val = self.dma_cnt[dma_sem]
        for k in reads:
            r = self.res.get(k)
            if r is None:
                self.res[k] = [None, [o]]
            else:
                r[1].append(o)
        for k in writes:
            self.res[k] = [o, []]
        return o

    def wait_only(self, eng, prods):
        o = Op(eng, lambda e: None)
        for pr in prods:
            self._need(o, pr)
        o.idx = len(self.q[eng])
        self.q[eng].append(o)
        return o

    def assign(self):
        for e in ENGS:
            c = 0
            for o in self.q[e]:
                if o.dma_sem is None and o.signal:
                    c += 1
                    o.semval = c

    def emit_engine(self, e, h, sems, dsems):
        for o in self.q[e]:
            for w in o.waits:
                if w.dma_sem is not None:
                    h.wait_ge(dsems[w.dma_sem], w.dma_val)
                else:
                    h.wait_ge(sems[w.eng], w.semval)
            inst = o.fn(h)
            if inst is None:
                continue
            if o.dma_sem is not None:
                inst.then_inc(dsems[o.dma_sem], 16)
            elif o.signal:
                inst.then_inc(sems[e], 1)


def _const_tables():
    slopes = np.exp2(-8.0 * np.arange(1, 9, dtype=np.float64) / 8.0)
    a = np.arange(128)[:, None]
    j = np.arange(640)[None, :]
    dl = a - j + 256
    ad = np.abs(dl)
    mult = (ad <= 64).astype(np.float64) + ((dl % 4 == 0) & (ad <= 256)).astype(np.float64)
    TA = np.empty((8, 128, 640), np.float32)
    for h in range(8):
        with np.errstate(divide="ignore"):
            bb = np.where(mult > 0, np.log(np.maximum(mult, 1e-30)) - slopes[h] * ad, NEG)
        TA[h] = bb.astype(np.float32)
    du = np.abs(np.arange(128)[:, None] - np.arange(128)[None, :])
    B3 = np.empty((8, 128, 128), np.float32)
    for h in range(8):
        B3[h] = np.where(du <= 64, -slopes[h] * 16.0 * du, NEG).astype(np.float32)
    half = 32
    pos = np.arange(T, dtype=np.float32)
    inv = np.power(np.float32(10000.0), -np.arange(half, dtype=np.float32) / half).astype(np.float32)
    ang = (pos[:, None] * inv[None, :]).astype(np.float32)
    cos = np.cos(ang).astype(np.float32).T
    sin = np.sin(ang).astype(np.float32).T
    cosT = np.concatenate([cos, cos], 0)
    sinT = np.concatenate([-sin, sin], 0)
    cosT, sinT = (np.ascontiguousarray(np.concatenate([cosT, cosT], 0)),
                  np.ascontiguousarray(np.concatenate([sinT, sinT], 0)))
    ident = np.eye(128, dtype=np.float32)
    return TA, B3, cosT, sinT, ident


def _prep_shared(w_ada, b_ada, g_pre, w_in, g_q_lora, w_uq, g_kv_lora, w_ukv, w_out, g_post):
    f = np.float32
    w_ada = np.asarray(w_ada, f)[0]
    b_ada = np.asarray(b_ada, f)[0]
    g_pre = np.asarray(g_pre, f)[0]
    w_in = np.asarray(w_in, f)[0]
    g_q = np.asarray(g_q_lora, f)[0]
    w_uq = np.asarray(w_uq, f)[0]
    g_kv = np.asarray(g_kv_lora, f)[0]
    w_ukv = np.asarray(w_ukv, f)[0]
    w_out = np.asarray(w_out, f)[0]
    g_post = np.asarray(g_post, f)[0]
    sh = {}
    sh["w_ada_l"] = np.ascontiguousarray(w_ada.reshape(16, 128, 12, 512).transpose(2, 1, 0, 3)).reshape(12, 128, 8192)
    sh["b_ss"] = np.ascontiguousarray(b_ada[0:4096].reshape(32, 128).T)
    sh["b_gate"] = np.ascontiguousarray(b_ada[4096:6144].reshape(1, 2048))
    sh["g_preT"] = np.ascontiguousarray(g_pre.reshape(16, 128).T)
    perm = []
    for h in range(8):
        for s in range(4):
            perm += list(range(s * 1024 + h * 128, s * 1024 + (h + 1) * 128))
    kpe = list(range(4864, 4928))
    kpe_sw = list(range(4896, 4928)) + list(range(4864, 4896))
    perm += list(range(4096, 4608)) + list(range(4608, 4864)) + kpe + kpe + kpe_sw + kpe_sw
    perm += list(range(4928, 5952))
    perm = np.asarray(perm)
    assert perm.size == NCH * 128
    wp = w_in[:, perm]
    sh["w_in_l"] = np.ascontiguousarray(wp.reshape(16, 128, NCH, 128).transpose(2, 1, 0, 3)).reshape(NCH, 128, 2048)
    pq = []
    for h in range(8):
        b0 = h * 192
        pq += list(range(b0, b0 + 128)) + list(range(b0 + 128, b0 + 192))
        pq += list(range(b0 + 160, b0 + 192)) + list(range(b0 + 128, b0 + 160))
    wq = w_uq[:, np.asarray(pq)]
    sh["w_uq_l"] = np.ascontiguousarray(wq.reshape(4, 128, 8, 256).transpose(2, 1, 0, 3)).reshape(8, 128, 1024)
    sh["w_ukv_l"] = np.ascontiguousarray(w_ukv.reshape(2, 128, 8, 256).transpose(2, 1, 0, 3)).reshape(8, 128, 512)
    sh["w_out_l"] = np.ascontiguousarray(w_out.reshape(16, 128, 2048).transpose(1, 0, 2)).reshape(128, 16 * 2048)
    sh["g_qT"] = np.ascontiguousarray(g_q.reshape(4, 128).T)
    sh["g_kvT"] = np.ascontiguousarray(g_kv.reshape(2, 128).T)
    sh["g_post"] = np.ascontiguousarray(g_post.reshape(1, 2048))
    TA, B3, cosT, sinT, ident = _const_tables()
    sh["TA"] = TA
    sh["B3"] = B3
    sh["cosT"] = cosT
    sh["sinT"] = sinT
    sh["csqT"] = np.ascontiguousarray(np.concatenate([cosT[0:64], sinT[0:64]], 0))
    sh["ident"] = ident
    return sh


_RC = {0: (0, 128), 1: (0, 256), 2: (0, 384), 3: (0, 512), 4: (0, 512), 5: (128, 512), 6: (256, 512), 7: (384, 512)}


def _run_merged(gp, np_, gs, ns):
    done_s = 0
    for i in range(np_):
        next(gp, None)
        target = ((i + 1) * ns) // np_ if np_ else ns
        while done_s < target:
            next(gs, None)
            done_s += 1
    for _ in gp:
        pass
    for _ in gs:
        pass


def build_nc(debug=False):
    nc = bass.Bass("TRN2", target_bir_lowering=False)

    def din(name, shape):
        return nc.dram_tensor(name, list(shape), F32, kind="ExternalInput").ap()

    x_d = din("x", [T, D])
    cT_d = din("cT", [128, 16])
    w_ada_d = din("w_ada_l", [12, 128, 16 * 512])
    b_ss_d = din("b_ss", [128, 32])
    b_gate_d = din("b_gate", [1, 2048])
    g_preT_d = din("g_preT", [128, 16])
    w_in_d = din("w_in_l", [NCH, 128, 16 * 128])
    w_uq_d = din("w_uq_l", [8, 128, 4 * 256])
    w_ukv_d = din("w_ukv_l", [8, 128, 2 * 256])
    w_out_d = din("w_out_l", [128, 16 * 2048])
    g_qT_d = din("g_qT", [128, 4])
    g_kvT_d = din("g_kvT", [128, 2])
    g_post_d = din("g_post", [1, 2048])
    TA_d = din("TA", [8, 128, 640])
    B3_d = din("B3", [8, 128, 128])
    cos_d = din("cosT", [128, T])
    sin_d = din("sinT", [128, T])
    csq_d = din("csqT", [128, T])
    ident_d = din("ident", [128, 128])
    out_d = nc.dram_tensor("out", [T, D], F32, kind="ExternalOutput").ap()
    dbg = {}
    if debug:
        dbg["hT"] = nc.dram_tensor("dbg_hT", [128, 16 * T], BF16, kind="ExternalOutput").ap()
        dbg["cqn"] = nc.dram_tensor("dbg_cqn", [128, 4 * T], BF16, kind="ExternalOutput").ap()
        dbg["ckvn"] = nc.dram_tensor("dbg_ckvn", [128, 2 * T], BF16, kind="ExternalOutput").ap()
        dbg["kpe"] = nc.dram_tensor("dbg_kpe", [64, T], BF16, kind="ExternalOutput").ap()
        dbg["yT"] = nc.dram_tensor("dbg_yT", [128, 16 * T], BF16, kind="ExternalOutput").ap()

    ARW = 52224
    with ExitStack() as es:
        arena = es.enter_context(nc.sbuf_tensor("arena", [128, ARW], F32))
        ps = [es.enter_context(nc.psum_tensor("ps%d" % i, [128, 512], F32)) for i in range(8)]
        A = arena[:]

        def v32(off, n):
            assert off % 4 == 0
            return A[:, off // 4: off // 4 + n]

        def v16(off, n):
            assert off % 4 == 0 and n % 2 == 0
            return A[:, off // 4: off // 4 + n // 2].bitcast(BF16)

        O_HT = 0
        O_YT = 65536
        O_LAT = 131072
        O_CQN = O_LAT
        O_CKVN = O_LAT + 16384
        O_KPE = O_LAT + 24576
        O_TR = 159744
        HT = v16(O_HT, 16 * T).rearrange("p (k t) -> p k t", k=16)
        YT = v16(O_YT, 16 * T).rearrange("p (k t) -> p k t", k=16)
        CQN = v16(O_CQN, 4 * T).rearrange("p (k t) -> p k t", k=4)
        CKVN = v16(O_CKVN, 2 * T).rearrange("p (k t) -> p k t", k=2)
        KPE = v16(O_KPE, T)
        O_ID = O_TR + 44032
        IDENT_BF = v16(O_ID, 128)
        ONES_BF = v16(O_ID + 256, 128)
        O_SM = O_TR + 44544
        SM = v32(O_SM, 256)
        c_sb = SM[:, 0:16]
        sc_f = SM[:, 16:32]
        MS = SM[:, 32:64]
        A_sc = SM[:, 64:80]
        g_pre_sb = SM[:, 80:96]
        b_ss_sb = SM[:, 96:128]
        g_q_sb = SM[:, 128:132]
        g_kv_sb = SM[:, 132:134]
        epsb = SM[:, 134:135]
        ssx = SM[:, 136:152]
        rstdx = SM[:, 152:168]
        rtmp = SM[:, 168:184]
        ssy = SM[:, 184:192]
        ssy1 = SM[:, 192:194]
        rty = SM[:, 194:196]
        rstdy = SM[:, 196:198]
        sc_bf = v16(O_SM + 200 * 4, 16)
        gateT = SM[:, 208:224]
        O_ID32 = O_TR + 45568
        IDENT_F = v32(O_ID32, 128)
        ONES_F = v32(O_ID32 + 512, 128)
        R2 = v32(O_TR + 46592, 512)
        B_sh = MS[:, 0:16]

        p = Prog()
        fin = []

        def ytk(off, n):
            return [("YT", i) for i in range(off // 4096, (off + n - 1) // 4096 + 1)]

        def latk(off, n):
            return [("LAT", i) for i in range(off // 4096, (off + n - 1) // 4096 + 1)]

        def trk(off, n):
            return [("TR", i) for i in range(off // 1024, (off + n - 1) // 1024 + 1)]

        p.op("pool", lambda e: e.dma_start(out=IDENT_BF, in_=ident_d), writes=["identbf"], dma_sem="c0")
        p.op("sp", lambda e: e.dma_start(out=IDENT_F, in_=ident_d), writes=["identf"], dma_sem="c1")
        p.op("sp", lambda e: e.dma_start(out=c_sb, in_=cT_d), writes=["c"], dma_sem="c2")
        p.op("sp", lambda e: e.dma_start(out=g_pre_sb, in_=g_preT_d), writes=["gpre"], dma_sem="c3")
        p.op("sp", lambda e: e.dma_start(out=b_ss_sb, in_=b_ss_d), writes=["bss"], dma_sem="c4")
        p.op("sp", lambda e: e.dma_start(out=g_q_sb, in_=g_qT_d), writes=["gq"], dma_sem="c5")
        p.op("sp", lambda e: e.dma_start(out=g_kv_sb, in_=g_kvT_d), writes=["gkv"], dma_sem="c6")
        p.op("dve", lambda e: e.memset(ONES_BF, 1.0), writes=["onesbf"])
        p.op("dve", lambda e: e.memset(ONES_F, 1.0), writes=["onesf"])
        p.op("dve", lambda e: e.memset(epsb, EPS), writes=["eps"])
        one_col = ONES_F[:, 0:1]
        p.op("act", lambda e: e.activation(out=sc_f, in_=c_sb, func=AF.Exp, scale=-1.0), reads=["c"], writes=["scf"])
        p.op("act", lambda e: e.activation(out=sc_f, in_=sc_f, func=AF.Ln, bias=one_col), reads=["scf", "onesf"], writes=["scf"])
        p.op("act", lambda e: e.activation(out=sc_f, in_=sc_f, func=AF.Exp, scale=-1.0), reads=["scf"], writes=["scf"])
        p.op("dve", lambda e: e.tensor_tensor(out=sc_f, in0=sc_f, in1=c_sb, op=ALU.mult), reads=["scf", "c"], writes=["scf"])
        p.op("dve", lambda e: e.tensor_copy(out=sc_bf, in_=sc_f), reads=["scf"], writes=["scbf"])

        WA = [v16(O_YT + s * 16384, 16 * 512).rearrange("p (k n) -> p k n", k=16) for s in range(2)]
        WAK = [ytk(s * 16384, 16384) for s in range(2)]
        NXT = 4
        XT = [v32(O_YT + 32768 + s * 8192, 2048) for s in range(NXT)]
        XTK = [ytk(32768 + s * 8192, 8192) for s in range(NXT)]
        JUNK = v16(O_LAT + 16384, 2048)
        JUNKK = latk(16384, 4096)
        DG = [v32(O_LAT + 20480 + s * 512, 128) for s in range(2)]

        def x_front(tb):
            s = tb % NXT
            s2 = tb % 2
            p.op("sp", lambda e: e.dma_start(out=XT[s], in_=x_d[tb * 128:(tb + 1) * 128, :]),
                 writes=XTK[s], dma_sem="xt%d" % s)
            p.op("act", lambda e: e.activation(out=JUNK, in_=XT[s], func=AF.Square, accum_out=ssx[:, tb:tb + 1]),
                 reads=XTK[s], writes=JUNKK + [("ssx", tb)])
            p.op("act", lambda e: e.activation(out=rtmp[:, tb:tb + 1], in_=ssx[:, tb:tb + 1], func=AF.Ln,
                                               bias=epsb, scale=1.0 / D),
                 reads=[("ssx", tb), "eps"], writes=[("rtmp", tb)])
            p.op("act", lambda e: e.activation(out=rstdx[:, tb:tb + 1], in_=rtmp[:, tb:tb + 1], func=AF.Exp, scale=-0.5),
                 reads=[("rtmp", tb)], writes=[("rstdx", tb)])
            p.op("dve", lambda e: e.tensor_scalar(out=DG[s2], in0=IDENT_F, scalar1=rstdx[:, tb:tb + 1], scalar2=1.0,
                                                  op0=ALU.mult, op1=ALU.mult),
                 reads=["identf", ("rstdx", tb)], writes=[("DG", s2)])
            for kc in range(16):
                b = 4 * s2 + kc // 4
                p.op("pe", lambda e, kc=kc, b=b: e.matmul(
                    ps[b][:, (kc % 4) * 128:(kc % 4 + 1) * 128], lhsT=XT[s][:, kc * 128:(kc + 1) * 128], rhs=DG[s2],
                    start=True, stop=True),
                    reads=XTK[s] + [("DG", s2)], writes=[("ps", b)])

        def x_back(tb):
            s = tb % 2
            for kc in range(16):
                b = 4 * s + kc // 4
                src = ps[b][:, (kc % 4) * 128:(kc % 4 + 1) * 128]
                dst = HT[:, kc, tb * 128:(tb + 1) * 128]
                if b % 4 != 3:
                    p.op("dve", lambda e, src=src, dst=dst, kc=kc: e.scalar_tensor_tensor(
                        out=dst, in0=src, scalar=A_sc[:, kc:kc + 1], in1=B_sh[:, kc:kc + 1].to_broadcast([128, 128]),
                        op0=ALU.mult, op1=ALU.add),
                        reads=[("ps", b), "Asc", "MS"], writes=[("HT", tb // 4, kc)])
                else:
                    p.op("act", lambda e, src=src, dst=dst, kc=kc: e.activation(
                        out=dst, in_=src, func=AF.Identity, bias=B_sh[:, kc:kc + 1], scale=A_sc[:, kc:kc + 1]),
                        reads=[("ps", b), "Asc", "MS"], writes=[("HT", tb // 4, kc)])

        psM = ps[7]
        for g in range(8):
            s = g % 2
            p.op("pool", lambda e, g=g, s=s: e.dma_start(out=WA[s].rearrange("p k n -> p (k n)"), in_=w_ada_d[g]),
                 writes=WAK[s], dma_sem="wa%d" % s)
            for sub in range(4):
                jn = 4 * g + sub
                for kc in range(16):
                    p.op("pe", lambda e, s=s, sub=sub, jn=jn, kc=kc: e.matmul(
                        psM[:, jn:jn + 1], lhsT=WA[s][:, kc, sub * 128:(sub + 1) * 128], rhs=sc_bf[:, kc:kc + 1],
                        start=(kc == 0), stop=(kc == 15)),
                        reads=WAK[s] + ["scbf"], writes=[("ps", 7)])
            if g == 5:
                x_front(0)
        p.op("dve", lambda e: e.tensor_tensor(out=MS, in0=psM[:, 0:32], in1=b_ss_sb, op=ALU.add),
             reads=[("ps", 7), "bss"], writes=["MS"])
        p.op("dve", lambda e: e.scalar_tensor_tensor(out=A_sc, in0=MS[:, 16:32], scalar=1.0, in1=g_pre_sb,
                                                      op0=ALU.add, op1=ALU.mult),
             reads=["MS", "gpre"], writes=["Asc"])
        for tb in range(16):
            if tb + 1 < 16:
                x_front(tb + 1)
            x_back(tb)
        HTK = [("HT", tc, kc) for tc in range(4) for kc in range(16)]

        NWIN = 2
        WIN = [v16(O_TR + i * 4096, 16 * 128).rearrange("p (k n) -> p k n", k=16) for i in range(NWIN)]
        WINK = [trk(i * 4096, 4096) for i in range(NWIN)]
        FA = v32(O_TR + 8192, 512)
        FB = v32(O_TR + 10240, 512)
        TA_sb = [v32(O_TR + 12288, 640)] * 2
        TAK = [["TA0"]] * 2
        B3_sb = [v32(O_TR + 14848, 128)] * 2
        B3K = [["B30"]] * 2
        TZ = v32(O_TR + 15360, 512)
        NPT = 5
        PT = [v16(O_TR + 18432 + i * 1024, 512) for i in range(NPT)]
        PTK = [["PT%d" % i] for i in range(NPT)]
        NTMP = 2
        TMP = [v32(O_TR + 23552 + i * 2048, 512) for i in range(NTMP)]
        TMPK = [["TMP%d" % i] for i in range(NTMP)]
        WG = v16(O_YT + 57344, 16 * 256).rearrange("p (k n) -> p k n", k=16)
        WGK = [("YT", 14), ("YT", 15)]
        U3S = v32(O_TR + 27648, 2048)
        L3S = v32(O_TR + 35840, 2048)
        U3SK = trk(27648, 8192)
        L3SK = trk(35840, 8192)

        def mkset(base, keyfn):
            d = {}
            names = ["QT", "KT", "VT", "VN", "V16", "GA"]
            for i, nm in enumerate(names):
                d[nm] = v16(base + i * 4096, T)
                d[nm + "K"] = keyfn(i)
            d["P3T"] = d["VT"].rearrange("p (r u) -> p r u", r=16)
            d["VN"] = d["VN"].rearrange("p (b d) -> p b d", b=16)
            d["V16"] = d["V16"].rearrange("p (r d) -> p r d", r=16)
            return d
        SETS = [mkset(O_LAT, lambda i: [("LAT", i)]), mkset(O_YT + 32768, lambda i: [("YT", 8 + i)])]

        tile_ctr = [0]
        ORDER = list(range(32)) + list(range(40, 48))
        pos_of = {j: i for i, j in enumerate(ORDER)}
        loaded = [0]

        def ensure_loaded(upto):
            while loaded[0] <= upto and loaded[0] < len(ORDER):
                pos = loaded[0]
                j = ORDER[pos]
                s = pos % NWIN
                p.op("pool", lambda e, s=s, j=j: e.dma_start(out=WIN[s].rearrange("p k n -> p (k n)"), in_=w_in_d[j]),
                     writes=WINK[s], dma_sem="win%d" % s)
                loaded[0] += 1

        nbanks = [2]

        def proj_tile(wt, wk, ncols, rhs_fn, nk, rkeys, m0=0, m1=128):
            b = tile_ctr[0] % nbanks[0]
            tile_ctr[0] += 1
            for kc in range(nk):
                p.op("pe", lambda e, kc=kc, b=b: e.matmul(
                    ps[b][0:(m1 - m0), 0:ncols], lhsT=wt[:, kc, m0:m1], rhs=rhs_fn(kc),
                    start=(kc == 0), stop=(kc == nk - 1)),
                    reads=wk + (rkeys(kc) if callable(rkeys) else rkeys), writes=[("ps", b)])
            return b

        win_slot = {}

        def ht_tile(j, tc, m0=0, m1=128):
            s = win_slot[j] if j in win_slot else pos_of[j] % NWIN
            return proj_tile(WIN[s], WINK[s], 512, lambda kc: HT[:, kc, tc * 512:(tc + 1) * 512], 16,
                             lambda kc: [("HT", tc, kc)], m0, m1)

        def silu_evac(b, dst, dstk):
            p.op("act", lambda e: e.activation(out=TZ, in_=ps[b][:, :], func=AF.Exp, scale=-1.0),
                 reads=[("ps", b)], writes=["TZ"])
            p.op("act", lambda e: e.activation(out=TZ, in_=TZ, func=AF.Ln, bias=one_col), reads=["TZ", "onesf"], writes=["TZ"])
            p.op("act", lambda e: e.activation(out=TZ, in_=TZ, func=AF.Exp, scale=-1.0), reads=["TZ"], writes=["TZ"])
            p.op("dve", lambda e: e.tensor_tensor(out=dst, in0=ps[b][:, :], in1=TZ, op=ALU.mult),
                 reads=[("ps", b), "TZ"], writes=dstk)

        def gen_inproj_head(h):
            S = SETS[(h + 1) % 2]
            hs = h % 2
            for sidx in range(4):
                j = 4 * h + sidx
                ensure_loaded(pos_of[j] + 1)
                for tc in range(4):
                    sl = slice(tc * 512, (tc + 1) * 512)
                    b = ht_tile(j, tc)
                    if sidx == 0:
                        p.op("dve", lambda e, b=b, sl=sl: e.tensor_copy(out=S["QT"][:, sl], in_=ps[b][:, :]),
                             reads=[("ps", b)], writes=S["QTK"])
                    elif sidx == 1:
                        p.op("act", lambda e, b=b, sl=sl: e.activation(out=S["KT"][:, sl], in_=ps[b][:, :], func=AF.Copy),
                             reads=[("ps", b)], writes=S["KTK"])
                    elif sidx == 2:
                        p.op("dve", lambda e, b=b, sl=sl: e.tensor_copy(out=S["VT"][:, sl], in_=ps[b][:, :]),
                             reads=[("ps", b)], writes=S["VTK"])
                    else:
                        silu_evac(b, S["GA"][:, sl], S["GAK"])
                    yield

        def gen_gate():
            def dma(g8):
                src = w_ada_d[8 + g8 // 2].rearrange("p (k n) -> p k n", k=16)[:, :, (g8 % 2) * 256:(g8 % 2 + 1) * 256]
                p.op("pool", lambda e, src=src: e.dma_start(out=WG, in_=src), writes=WGK, dma_sem="wg")

            def mm(g8):
                b = tile_ctr[0] % 2
                tile_ctr[0] += 1
                for sub in range(2):
                    for kc in range(16):
                        p.op("pe", lambda e, sub=sub, kc=kc, b=b: e.matmul(
                            ps[b][:, sub:sub + 1], lhsT=WG[:, kc, sub * 128:(sub + 1) * 128], rhs=sc_bf[:, kc:kc + 1],
                            start=(kc == 0), stop=(kc == 15)),
                            reads=WGK + ["scbf"], writes=[("ps", b)])
                p.op("dve", lambda e, b=b, g8=g8: e.tensor_copy(out=gateT[:, 2 * g8:2 * g8 + 2], in_=ps[b][:, 0:2]),
                     reads=[("ps", b)], writes=[("gateT", g8)])

            dma(0)
            yield
            for g8 in range(1, 8):
                mm(g8 - 1)
                dma(g8)
                yield
            mm(7)
            yield

        def gen_attn(h):
            S = SETS[(h + 1) % 2]
            hs = h % 2
            p.op("sp", lambda e: e.dma_start(out=TA_sb[hs], in_=TA_d[h]), writes=TAK[hs], dma_sem="ta0")
            p.op("sp", lambda e: e.dma_start(out=B3_sb[hs], in_=B3_d[h]), writes=B3K[hs], dma_sem="b30")
            QT, KT, VT, P3T, VN, V16, GA = S["QT"], S["KT"], S["VT"], S["P3T"], S["VN"], S["V16"], S["GA"]
            QTK, KTK, VTK, VNK, V16K, GAK = S["QTK"], S["KTK"], S["VTK"], S["VNK"], S["V16K"], S["GAK"]
            cnt = [0]
            for which in range(2):
                for g in range(4):
                    sb = 2 + cnt[0] % 6
                    cnt[0] += 1
                    for bi in range(4):
                        idx = 4 * g + bi
                        src = VT[:, idx * 128:(idx + 1) * 128] if which == 0 else VT[:, idx:T:16]
                        p.op("pe", lambda e, sb=sb, bi=bi, src=src: e.matmul(ps[sb][:, bi * 128:(bi + 1) * 128], lhsT=src,
                                                                          rhs=IDENT_BF, start=True, stop=True),
                             reads=VTK + ["identbf"], writes=[("ps", sb)])
                    dst = (VN if which == 0 else V16)[:, 4 * g:4 * g + 4, :]
                    dk = VNK if which == 0 else V16K
                    srcp = ps[sb][:, :].rearrange("p (b d) -> p b d", b=4)
                    if g % 2 == 0:
                        p.op("dve", lambda e, dst=dst, srcp=srcp: e.tensor_copy(out=dst, in_=srcp),
                             reads=[("ps", sb)], writes=dk)
                    else:
                        p.op("act", lambda e, dst=dst, srcp=srcp: e.activation(out=dst, in_=srcp, func=AF.Copy),
                             reads=[("ps", sb)], writes=dk)
                    yield
            for g in range(4):
                sb = 2 + cnt[0] % 6
                ti = cnt[0] % NTMP
                cnt[0] += 1
                for ri in range(4):
                    r = 4 * g + ri
                    p.op("pe", lambda e, sb=sb, ri=ri, r=r: e.matmul(ps[sb][:, ri * 128:(ri + 1) * 128],
                                                                    lhsT=KT[:, r:T:16], rhs=QT[:, r:T:16],
                                                                    start=True, stop=True),
                         reads=KTK + QTK, writes=[("ps", sb)])
                p.op("dve", lambda e, sb=sb, ti=ti: e.scalar_tensor_tensor(
                    out=TMP[ti].rearrange("p (r u) -> p r u", r=4), in0=ps[sb][:, :].rearrange("p (r u) -> p r u", r=4),
                    scalar=SC_A, in1=B3_sb[hs].unsqueeze(1).to_broadcast([128, 4, 128]), op0=ALU.mult, op1=ALU.add),
                    reads=[("ps", sb)] + B3K[hs], writes=TMPK[ti])
                p.op("act", lambda e, g=g, ti=ti: e.activation(out=P3T[:, 4 * g:4 * g + 4, :],
                                                              in_=TMP[ti].rearrange("p (r u) -> p r u", r=4), func=AF.Exp),
                     reads=TMPK[ti], writes=VTK)
                yield
            U3v = U3S.rearrange("p (u r) -> p r u", r=16)
            L3v = L3S.rearrange("p (u r) -> p r u", r=16)
            for g in range(4):
                sbu = 2 + cnt[0] % 6
                sbl = 2 + (cnt[0] + 1) % 6
                cnt[0] += 2
                for ri in range(4):
                    r = 4 * g + ri
                    p.op("pe", lambda e, sbu=sbu, ri=ri, r=r: e.matmul(ps[sbu][:, ri * 128:(ri + 1) * 128],
                                                                      lhsT=V16[:, r, :], rhs=P3T[:, r, :],
                                                                      start=True, stop=True),
                         reads=V16K + VTK, writes=[("ps", sbu)])
                for ri in range(4):
                    r = 4 * g + ri
                    p.op("pe", lambda e, sbl=sbl, ri=ri, r=r: e.matmul(ps[sbl][:, ri * 128:(ri + 1) * 128],
                                                                      lhsT=ONES_BF, rhs=P3T[:, r, :],
                                                                      start=True, stop=True),
                         reads=["onesbf"] + VTK, writes=[("ps", sbl)])
                p.op("dve", lambda e, sbu=sbu, g=g: e.tensor_copy(
                    out=U3v[:, 4 * g:4 * g + 4, :], in_=ps[sbu][:, :].rearrange("p (r u) -> p r u", r=4)),
                    reads=[("ps", sbu)], writes=U3SK)
                p.op("act", lambda e, sbl=sbl, g=g: e.activation(
                    out=L3v[:, 4 * g:4 * g + 4, :], in_=ps[sbl][:, :].rearrange("p (r u) -> p r u", r=4), func=AF.Copy),
                    reads=[("ps", sbl)], writes=L3SK)
                yield
            defer = []

            def flush(keep=0):
                while len(defer) > keep:
                    defer.pop(0)()

            for qc in range(4):
                order = [c for c in (3, 4, 2, 5, 1, 6, 0, 7) if 0 <= 4 * qc - 2 + c <= 15]
                for si, c in enumerate(order):
                    kb = 4 * qc - 2 + c
                    i0, i1 = _RC[c]
                    n = i1 - i0
                    j0 = i0 + 512 - 128 * c
                    sb = 2 + cnt[0] % 4
                    pi = cnt[0] % NPT
                    ti = cnt[0] % NTMP
                    cnt[0] += 1
                    p.op("pe", lambda e, sb=sb, kb=kb, i0=i0, i1=i1, qc=qc: e.matmul(
                        ps[sb][:, i0:i1], lhsT=KT[:, kb * 128:(kb + 1) * 128], rhs=QT[:, qc * 512 + i0:qc * 512 + i1],
                        start=True, stop=True),
                        reads=KTK + QTK, writes=[("ps", sb)])
                    flush(3)
                    p.op("dve", lambda e, sb=sb, ti=ti, i0=i0, i1=i1, j0=j0, n=n: e.scalar_tensor_tensor(
                        out=TMP[ti][:, i0:i1], in0=ps[sb][:, i0:i1], scalar=SC_A, in1=TA_sb[hs][:, j0:j0 + n],
                        op0=ALU.mult, op1=ALU.add),
                        reads=[("ps", sb)] + TAK[hs], writes=TMPK[ti])
                    p.op("act", lambda e, pi=pi, ti=ti, i0=i0, i1=i1: e.activation(out=PT[pi][:, i0:i1],
                                                                                  in_=TMP[ti][:, i0:i1], func=AF.Exp),
                         reads=TMPK[ti], writes=PTK[pi])

                    def pv(kb=kb, pi=pi, i0=i0, i1=i1, first=(si == 0), last=(si == len(order) - 1), bu=6):
                        p.op("pe", lambda e: e.matmul(ps[bu][:, i0:i1], lhsT=VN[:, kb, :], rhs=PT[pi][:, i0:i1],
                                                      start=first, stop=last, skip_group_check=True),
                             reads=VNK + PTK[pi], writes=[("ps", bu)])
                        p.op("pe", lambda e: e.matmul(ps[bu + 1][:, i0:i1], lhsT=ONES_BF, rhs=PT[pi][:, i0:i1],
                                                      start=first, stop=last, skip_group_check=True),
                             reads=["onesbf"] + PTK[pi], writes=[("ps", bu + 1)])
                    defer.append(pv)
                    yield
                flush(0)
                qs = slice(qc * 512, (qc + 1) * 512)
                bu = 6
                p.op("dve", lambda e, qs=qs, bu=bu: e.tensor_tensor(out=FA, in0=ps[bu + 1][:, :], in1=L3S[:, qs], op=ALU.add),
                     reads=[("ps", bu + 1)] + L3SK, writes=["FA"])
                p.op("dve", lambda e, qs=qs, bu=bu: e.tensor_tensor(out=FB, in0=ps[bu][:, :], in1=U3S[:, qs], op=ALU.add),
                     reads=[("ps", bu)] + U3SK, writes=["FB"])
                p.op("act", lambda e: e.activation(out=FA, in_=FA, func=AF.Ln), reads=["FA"], writes=["FA"])
                p.op("act", lambda e: e.activation(out=FA, in_=FA, func=AF.Exp, scale=-1.0), reads=["FA"], writes=["FA"])
                p.op("pool", lambda e: e.tensor_tensor(out=FB, in0=FB, in1=FA, op=ALU.mult),
                     reads=["FA", "FB"], writes=["FB"])
                p.op("pool", lambda e, qs=qs: e.tensor_tensor(out=YT[:, h, qs], in0=FB, in1=GA[:, qs], op=ALU.mult),
                     reads=["FB"] + GAK, writes=[("YT", h)])
                yield

        N_ATT = 8 + 4 + 4 + 4 * 1 + sum(len([c for c in range(8) if 0 <= 4 * qc - 2 + c <= 15]) for qc in range(4))

        def gen_bz():
            for h in range(8):
                j = 40 + h
                ensure_loaded(pos_of[j] + 1)
                for tc in range(4):
                    b = ht_tile(j, tc)
                    silu_evac(b, YT[:, 8 + h, tc * 512:(tc + 1) * 512], [("YT", 8 + h)])
                    yield

        ensure_loaded(1)
        for _ in gen_inproj_head(0):
            pass
        gate_gen = gen_gate()

        def attn_with_gate(h, every):
            for i, _ in enumerate(gen_attn(h)):
                if i % every == every - 1:
                    next(gate_gen, None)
                yield

        for h in range(8):
            if h < 7:
                sec = attn_with_gate(h, N_ATT // 5) if h in (1, 2) else gen_attn(h)
                _run_merged(gen_inproj_head(h + 1), 16, sec, N_ATT)
            else:
                _run_merged(gen_bz(), 32, gen_attn(h), N_ATT)
        for _ in gate_gen:
            pass

        p.barrier()
        O_LT = O_TR + 8192
        CQRAW = [v32(O_LT + j * 8192, 2048) for j in range(4)]
        defer = []

        def flush():
            while defer:
                defer.pop(0)()

        def evac_latent(tc, b, raw, rawk, SQl, SQK, first, last, ctr):
            sl = slice(tc * 512, (tc + 1) * 512)
            p.op("act", lambda e: e.activation(out=raw[:, sl], in_=ps[b][:, :], func=AF.Copy),
                 reads=[("ps", b)], writes=rawk)
            i = ctr[0] % 2
            ctr[0] += 1
            p.op("act", lambda e: e.activation(out=SQl[i], in_=ps[b][:, :], func=AF.Square),
                 reads=[("ps", b)], writes=SQK[i])
            defer.append(lambda: p.op("pe", lambda e: e.matmul(
                ps[2 + tc][:, :], lhsT=ONES_F, rhs=SQl[i], start=first, stop=last),
                reads=SQK[i] + ["onesf"], writes=[("ps", 2 + tc)]))

        def norm_latent(raws, rawks, nch, gsb, dst, dstk, nfeat, RSl, RSK):
            for tc in range(4):
                i = tc % 2
                p.op("act", lambda e, tc=tc, i=i: e.activation(out=RSl[i], in_=ps[2 + tc][:, :], func=AF.Ln,
                                                               bias=epsb, scale=1.0 / nfeat),
                     reads=[("ps", 2 + tc), "eps"], writes=RSK[i])
                p.op("act", lambda e, i=i: e.activation(out=RSl[i], in_=RSl[i], func=AF.Exp, scale=-0.5),
                     reads=RSK[i], writes=RSK[i])
                for jj in range(nch):
                    p.op("dve", lambda e, jj=jj, tc=tc, i=i: e.scalar_tensor_tensor(
                        out=dst[:, jj, tc * 512:(tc + 1) * 512], in0=raws[jj][:, tc * 512:(tc + 1) * 512],
                        scalar=gsb[:, jj:jj + 1], in1=RSl[i], op0=ALU.mult, op1=ALU.mult),
                        reads=rawks[jj] + RSK[i] + ["gq", "gkv"], writes=dstk(jj))

        lat_loaded = [32]

        def load_lat(j):
            while lat_loaded[0] <= j and lat_loaded[0] <= 39:
                jj_ = lat_loaded[0]
                s = jj_ % 2
                win_slot[jj_] = s
                p.op("pool", lambda e, s=s, jj_=jj_: e.dma_start(out=WIN[s].rearrange("p k n -> p (k n)"), in_=w_in_d[jj_]),
                     writes=WINK[s], dma_sem="win%d" % s)
                lat_loaded[0] += 1

        load_lat(33)
        SQa = [v32(O_CKVN + i * 2048, 512) for i in range(2)]
        RSa = [v32(O_CKVN + 4096 + i * 2048, 512) for i in range(2)]
        SQaK = [["SQa%d" % i] for i in range(2)]
        RSaK = [["RSa%d" % i] for i in range(2)]
        CQRAWK = [["CQRAW%d" % j] for j in range(4)]
        ctr = [0]
        for jj in range(4):
            j = 32 + jj
            load_lat(j + 1)
            for tc in range(4):
                b = ht_tile(j, tc)
                flush()
                evac_latent(tc, b, CQRAW[jj], CQRAWK[jj], SQa, SQaK, jj == 0, jj == 3, ctr)
        flush()
        norm_latent(CQRAW, CQRAWK, 4, g_q_sb, CQN, lambda jj: [("LAT", jj)], 512, RSa, RSaK)
        p.barrier()
        CKRAW = [v32(O_LT + j * 8192, 2048) for j in range(2)]
        CKRAWK = [["CKRAW%d" % j] for j in range(2)]
        SQb = [v32(O_LT + 16384 + i * 2048, 512) for i in range(2)]
        RSb = [v32(O_LT + 20480 + i * 2048, 512) for i in range(2)]
        SQbK = [["SQb%d" % i] for i in range(2)]
        RSbK = [["RSb%d" % i] for i in range(2)]
        for jj in range(2):
            j = 36 + jj
            load_lat(j + 1)
            for tc in range(4):
                b = ht_tile(j, tc)
                flush()
                evac_latent(tc, b, CKRAW[jj], CKRAWK[jj], SQb, SQbK, jj == 0, jj == 1, ctr)
        flush()
        norm_latent(CKRAW, CKRAWK, 2, g_kv_sb, CKVN, lambda jj: [("LAT", 4 + jj)], 256, RSb, RSbK)
        p.barrier()
        COS1 = v32(O_LT, 2048)
        SIN1 = v32(O_LT + 8192, 2048)
        T1 = v32(O_LT + 16384, 512)
        T2 = v32(O_LT + 18432, 512)
        p.op("sp", lambda e: e.dma_start(out=COS1, in_=cos_d), writes=["COS1"], dma_sem="cos")
        p.op("sp", lambda e: e.dma_start(out=SIN1, in_=sin_d), writes=["SIN1"], dma_sem="sin")
        load_lat(39)
        for tc in range(4):
            sl = slice(tc * 512, (tc + 1) * 512)
            ba = ht_tile(38, tc)
            bb = ht_tile(39, tc)
            p.op("dve", lambda e, ba=ba, sl=sl: e.tensor_tensor(out=T1, in0=ps[ba][:, :], in1=COS1[:, sl], op=ALU.mult),
                 reads=[("ps", ba), "COS1"], writes=["T1"])
            p.op("dve", lambda e, bb=bb, sl=sl: e.tensor_tensor(out=T2, in0=ps[bb][:, :], in1=SIN1[:, sl], op=ALU.mult),
                 reads=[("ps", bb), "SIN1"], writes=["T2"])
            p.op("dve", lambda e, sl=sl: e.tensor_tensor(out=KPE[:, sl], in0=T1, in1=T2, op=ALU.add),
                 reads=["T1", "T2"], writes=[("LAT", 6)])

        if debug:
            fin.append(p.op("sp", lambda e: e.dma_start(out=dbg["hT"], in_=HT.rearrange("p k t -> p (k t)")),
                            reads=HTK, dma_sem="dbg0"))
            fin.append(p.op("sp", lambda e: e.dma_start(out=dbg["cqn"], in_=CQN.rearrange("p k t -> p (k t)")),
                            reads=[("LAT", i) for i in range(4)], dma_sem="dbg1"))
            fin.append(p.op("sp", lambda e: e.dma_start(out=dbg["ckvn"], in_=CKVN.rearrange("p k t -> p (k t)")),
                            reads=[("LAT", 4), ("LAT", 5)], dma_sem="dbg2"))
            fin.append(p.op("sp", lambda e: e.dma_start(out=dbg["kpe"], in_=KPE[0:64, :]), reads=[("LAT", 6)],
                            dma_sem="dbg3"))

        p.barrier()
        WOUT = v16(O_HT, 16 * 2048).rearrange("p (k n) -> p k n", k=16)
        WUQ = v16(O_TR + 0, 4 * 256).rearrange("p (k n) -> p k n", k=4)
        WUKV = v16(O_TR + 2048, 2 * 256).rearrange("p (k n) -> p k n", k=2)
        QN = v16(O_TR + 4096, T)
        KN = v16(O_TR + 8192, T)
        QPE = v16(O_TR + 12288, T)
        VB = v16(O_TR + 16384, T).rearrange("p (b d) -> p b d", b=16)
        PT2 = [v16(O_TR + 20480 + i * 1024, 512) for i in range(6)]
        R1 = v32(O_TR + 26624, 512)
        CSQ = v32(O_HT + 49152, 2048)
        CQNK = [("LAT", i) for i in range(4)]
        CKVNK = [("LAT", 4), ("LAT", 5)]
        KPEK = [("LAT", 6)]

        def load_wb(h):
            p.op("pool", lambda e: e.dma_start(out=WUQ.rearrange("p k n -> p (k n)"), in_=w_uq_d[h]),
                 writes=["WUQ"], dma_sem="wuq")
            p.op("pool", lambda e: e.dma_start(out=WUKV.rearrange("p k n -> p (k n)"), in_=w_ukv_d[h]),
                 writes=["WUKV"], dma_sem="wukv")

        load_wb(0)
        p.op("sp", lambda e: e.dma_start(out=CSQ, in_=csq_d), writes=[("WOUT", 3)], dma_sem="cos")

        def load_wout(g):
            p.op("pool", lambda e: e.dma_start(
                out=WOUT[:, 4 * g:4 * g + 4, :].rearrange("p k n -> p (k n)"),
                in_=w_out_d[:, g * 8192:(g + 1) * 8192]),
                writes=[("WOUT", g)], dma_sem="wout%d" % g)

        nbanks[0] = 4
        for h in range(8):
            for tc in range(4):
                sl = slice(tc * 512, (tc + 1) * 512)
                b = proj_tile(WUKV, ["WUKV"], 512, lambda kc, sl=sl: CKVN[:, kc, sl], 2, CKVNK, 0, 128)
                p.op("act", lambda e, b=b, sl=sl: e.activation(out=KN[:, sl], in_=ps[b][:, :], func=AF.Copy),
                     reads=[("ps", b)], writes=["KN"])
            for tc in range(4):
                sl = slice(tc * 512, (tc + 1) * 512)
                b = proj_tile(WUQ, ["WUQ"], 512, lambda kc, sl=sl: CQN[:, kc, sl], 4, CQNK, 0, 128)
                p.op("dve", lambda e, b=b, sl=sl: e.tensor_copy(out=QN[:, sl], in_=ps[b][:, :]),
                     reads=[("ps", b)], writes=["QN"])
                ba = proj_tile(WUQ, ["WUQ"], 512, lambda kc, sl=sl: CQN[:, kc, sl], 4, CQNK, 128, 256)
                p.op("dve", lambda e, ba=ba, sl=sl: e.tensor_tensor(out=QPE[:, sl], in0=ps[ba][:, :], in1=CSQ[:, sl],
                                                                    op=ALU.mult),
                     reads=[("ps", ba), ("WOUT", 3)], writes=["QPE"])
            for g in range(4):
                b = tile_ctr[0] % nbanks[0]
                tile_ctr[0] += 1
                for bi in range(4):
                    blk = 4 * g + bi
                    for kc in range(2):
                        p.op("pe", lambda e, b=b, bi=bi, blk=blk, kc=kc: e.matmul(
                            ps[b][:, bi * 128:(bi + 1) * 128], lhsT=CKVN[:, kc, blk * 128:(blk + 1) * 128],
                            rhs=WUKV[:, kc, 128:256], start=(kc == 0), stop=(kc == 1)),
                            reads=CKVNK + ["WUKV"], writes=[("ps", b)])
                p.op("dve", lambda e, b=b, g=g: e.tensor_copy(
                    out=VB[:, 4 * g:4 * g + 4, :], in_=ps[b][:, :].rearrange("p (b d) -> p b d", b=4)),
                    reads=[("ps", b)], writes=["VB"])
            if h < 7:
                load_wb(h + 1)
            if h < 3:
                load_wout(h)
            if h == 7:
                load_wout(3)
            defer = []

            def flush2(keep=0):
                while len(defer) > keep:
                    defer.pop(0)()

            cnt = 0
            for qc in range(4):
                qs = slice(qc * 512, (qc + 1) * 512)
                for kb in range(16):
                    sb = 2 + cnt % 2
                    pi = cnt % 6
                    cnt += 1
                    ks = slice(kb * 128, (kb + 1) * 128)
                    p.op("pe", lambda e, sb=sb, ks=ks, qs=qs: e.matmul(ps[sb][:, :], lhsT=KN[:, ks], rhs=QN[:, qs],
                                                                      start=True, stop=False),
                         reads=["KN", "QN"], writes=[("ps", sb)])
                    p.op("pe", lambda e, sb=sb, ks=ks, qs=qs: e.matmul(ps[sb][:, :], lhsT=KPE[:, ks], rhs=QPE[:, qs],
                                                                      start=False, stop=True),
                         reads=KPEK + ["QPE"], writes=[("ps", sb)])
                    flush2(2)
                    p.op("act", lambda e, sb=sb, pi=pi: e.activation(out=PT2[pi], in_=ps[sb][:, :], func=AF.Exp,
                                                                    scale=SC_B),
                         reads=[("ps", sb)], writes=[("PT2", pi)])

                    def pv(kb=kb, pi=pi, bu=4 + 2 * (qc % 2)):
                        p.op("pe", lambda e: e.matmul(ps[bu][:, :], lhsT=VB[:, kb, :], rhs=PT2[pi],
                                                      start=(kb == 0), stop=(kb == 15)),
                             reads=["VB", ("PT2", pi)], writes=[("ps", bu)])
                        p.op("pe", lambda e: e.matmul(ps[bu + 1][:, :], lhsT=ONES_BF, rhs=PT2[pi],
                                                      start=(kb == 0), stop=(kb == 15)),
                             reads=["onesbf", ("PT2", pi)], writes=[("ps", bu + 1)])
                    defer.append(pv)
                yv = YT[:, 8 + h, qs]
                bu = 4 + 2 * (qc % 2)

                def fin_qc(yv=yv, bu=bu):
                    p.op("act", lambda e: e.activation(out=R1, in_=ps[bu + 1][:, :], func=AF.Ln),
                         reads=[("ps", bu + 1)], writes=["R1"])
                    p.op("act", lambda e: e.activation(out=R1, in_=R1, func=AF.Exp, scale=-1.0), reads=["R1"], writes=["R1"])
                    p.op("dve", lambda e: e.tensor_tensor(out=R2, in0=ps[bu][:, :], in1=R1, op=ALU.mult),
                         reads=[("ps", bu), "R1"], writes=["R2"])
                    p.op("pool", lambda e: e.tensor_tensor(out=yv, in0=R2, in1=yv, op=ALU.mult),
                         reads=["R2", ("YT", 8 + h)], writes=[("YT", 8 + h)])
                defer.append(fin_qc)
            flush2(0)
            if False:
                p.op("dve", lambda e: e.reciprocal(out=R1, in_=ps[7][:, :]), reads=[("ps", 7)], writes=["R1"])
                p.op("dve", lambda e: e.tensor_tensor(out=R2, in0=ps[6][:, :], in1=R1, op=ALU.mult),
                     reads=[("ps", 6), "R1"], writes=["R2"])
                p.op("dve", lambda e, yv=yv: e.tensor_tensor(out=yv, in0=R2, in1=yv, op=ALU.mult),
                     reads=["R2", ("YT", 8 + h)], writes=[("YT", 8 + h)])

        if debug:
            fin.append(p.op("sp", lambda e: e.dma_start(out=dbg["yT"], in_=YT.rearrange("p k t -> p (k t)")),
                            reads=[("YT", i) for i in range(16)], dma_sem="dbg4"))

        p.barrier()
        RG = v32(O_LAT, 2048)
        GG = v32(O_LAT + 8192, 2048)
        BG = v32(O_LAT + 16384, 2048)
        GP = v32(O_TR + 0, 2048)
        p.op("sp", lambda e: e.dma_start(out=BG, in_=b_gate_d.partition_broadcast(128)), writes=["BG"], dma_sem="c2")
        p.op("sp", lambda e: e.dma_start(out=GP, in_=g_post_d.partition_broadcast(128)), writes=["GP"], dma_sem="c3")
        for kc in range(16):
            p.op("dve", lambda e, kc=kc: e.tensor_scalar(out=RG[:, kc * 128:(kc + 1) * 128], in0=IDENT_F,
                                                         scalar1=gateT[:, kc:kc + 1], scalar2=1.0,
                                                         op0=ALU.mult, op1=ALU.mult),
                 reads=["identf"] + [("gateT", kc // 2)], writes=[("RG", kc // 4)])
        for g in range(4):
            gs = slice(g * 512, (g + 1) * 512)
            p.op("pe", lambda e, g=g, gs=gs: e.matmul(ps[g][:, :], lhsT=ONES_F, rhs=RG[:, gs], start=True, stop=True),
                 reads=["onesf", ("RG", g)], writes=[("ps", g)])
            p.op("dve", lambda e, g=g, gs=gs: e.tensor_tensor(out=GG[:, gs], in0=ps[g][:, :], in1=BG[:, gs], op=ALU.add),
                 reads=[("ps", g), "BG"], writes=[("GG", g)])
            p.op("dve", lambda e, gs=gs: e.tensor_tensor(out=GG[:, gs], in0=GG[:, gs], in1=GP[:, gs], op=ALU.mult),
                 reads=[("GG", g), "GP"], writes=[("GG", g)])
        XR = [v32(O_TR + 8192 + s * 8192, 2048) for s in range(2)]
        RES = [v32(O_TR + 24576 + s * 8192, 2048) for s in range(2)]
        JK = v16(O_TR + 40960, 512)
        outs = []
        YTK = [("YT", i) for i in range(16)]
        for tb in range(16):
            s = tb % 2
            ts_ = slice(tb * 128, (tb + 1) * 128)
            p.op("sp", lambda e, s=s, ts_=ts_: e.dma_start(out=XR[s], in_=x_d[ts_, :]), writes=[("XR", s)],
                 dma_sem="xt%d" % s)
            for nb in range(4):
                b = 4 * s + nb
                for kc in range(16):
                    p.op("pe", lambda e, b=b, kc=kc, nb=nb, ts_=ts_: e.matmul(
                        ps[b][:, :], lhsT=YT[:, kc, ts_], rhs=WOUT[:, kc, nb * 512:(nb + 1) * 512],
                        start=(kc == 0), stop=(kc == 15)),
                        reads=YTK + [("WOUT", kc // 4)], writes=[("ps", b)])
            for nb in range(4):
                b = 4 * s + nb
                p.op("act", lambda e, b=b, s=s, nb=nb: e.activation(out=JK, in_=ps[b][:, :], func=AF.Square,
                                                                    accum_out=ssy[:, 4 * s + nb:4 * s + nb + 1]),
                     reads=[("ps", b)], writes=["JK", ("ssy", s, nb)])
            p.op("dve", lambda e, s=s: e.reduce_sum(out=ssy1[:, s:s + 1], in_=ssy[:, 4 * s:4 * s + 4], axis=AX.X),
                 reads=[("ssy", s, nb) for nb in range(4)], writes=[("ssy1", s)])
            p.op("act", lambda e, s=s: e.activation(out=rty[:, s:s + 1], in_=ssy1[:, s:s + 1], func=AF.Ln,
                                                    bias=epsb, scale=1.0 / D),
                 reads=[("ssy1", s), "eps"], writes=[("rty", s)])
            p.op("act", lambda e, s=s: e.activation(out=rstdy[:, s:s + 1], in_=rty[:, s:s + 1], func=AF.Exp, scale=-0.5),
                 reads=[("rty", s)], writes=[("rstdy", s)])
            for nb in range(4):
                b = 4 * s + nb
                ns = slice(nb * 512, (nb + 1) * 512)
                p.op("dve", lambda e, b=b, s=s, ns=ns: e.scalar_tensor_tensor(
                    out=RES[s][:, ns], in0=ps[b][:, :], scalar=rstdy[:, s:s + 1], in1=GG[:, ns],
                    op0=ALU.mult, op1=ALU.mult),
                    reads=[("ps", b), ("rstdy", s)] + [("GG", nb)], writes=[("RES", s, nb)])
            p.op("pool", lambda e, s=s: e.tensor_tensor(out=RES[s], in0=RES[s], in1=XR[s], op=ALU.add),
                 reads=[("RES", s, nb) for nb in range(4)] + [("XR", s)], writes=[("RES", s, nb) for nb in range(4)])
            outs.append(p.op("sp", lambda e, s=s, ts_=ts_: e.dma_start(out=out_d[ts_, :], in_=RES[s]),
                             reads=[("RES", s, nb) for nb in range(4)], dma_sem="out%d" % s))
        p.wait_only("sp", outs[-2:] + fin)
        p.assign()

        sems = {e: es.enter_context(nc.semaphore("s_" + e)) for e in ENGS}
        dsems = {n: es.enter_context(nc.semaphore("d_" + n)) for n in p.dma_names}
        block = es.enter_context(nc.Block())

        @block.tensor
        def _(e):
            p.emit_engine("pe", e, sems, dsems)

        @block.scalar
        def _(e):
            p.emit_engine("act", e, sems, dsems)

        @block.vector
        def _(e):
            p.emit_engine("dve", e, sems, dsems)

        @block.gpsimd
        def _(e):
            p.emit_engine("pool", e, sems, dsems)

        @block.sync
        def _(e):
            p.emit_engine("sp", e, sems, dsems)
    return nc


_NC_CACHE = {}


def kernel(x, c, w_ada, b_ada, g_pre, w_in, g_q_lora, w_uq, g_kv_lora, w_ukv, w_out, g_post):
    x = np.asarray(x, np.float32)
    c = np.asarray(c, np.float32)
    sh = _prep_shared(w_ada, b_ada, g_pre, w_in, g_q_lora, w_uq, g_kv_lora, w_ukv, w_out, g_post)
    if "nc" not in _NC_CACHE:
        _NC_CACHE["nc"] = build_nc()
    nc = _NC_CACHE["nc"]
    in_maps = []
    for b in range(8):
        m = dict(sh)
        m["x"] = np.ascontiguousarray(x[b])
        m["cT"] = np.ascontiguousarray(c[b].reshape(16, 128).T)
        in_maps.append(m)
    res = run_bass_kernel_spmd(nc, in_maps, core_ids=list(range(8)))
    return np.stack([np.asarray(r["out"], np.float32) for r in res.results], axis=0)
```

```python
import numpy as np
from contextlib import ExitStack
import concourse.bass as bass
import concourse.mybir as mybir
from concourse.bass_utils import run_bass_kernel_spmd

F32 = mybir.dt.float32
BF16 = mybir.dt.bfloat16
AF = mybir.ActivationFunctionType
ALU = mybir.AluOpType
AX = mybir.AxisListType

T = 2048
D = 2048
NCH = 48
EPS = 1e-6
SC_A = 128.0 ** -0.5
SC_B = 192.0 ** -0.5
NEG = -30000.0

ENGS = ("pe", "act", "dve", "pool", "sp")


class Op:
    __slots__ = ("eng", "fn", "waits", "signal", "semval", "idx", "dma_sem", "dma_val")

    def __init__(self, eng, fn):
        self.eng = eng
        self.fn = fn
        self.waits = []
        self.signal = False
        self.semval = None
        self.idx = None
        self.dma_sem = None
        self.dma_val = None


class Prog:
    def __init__(self):
        self.q = {e: [] for e in ENGS}
        self.res = {}
        self.waited = {e: {} for e in ENGS}
        self.dma_cnt = {}
        self.dma_names = []
        self.pending_barrier = {e: [] for e in ENGS}

    def _need(self, op, prod):
        if prod is None:
            return
        if prod.dma_sem is not None:
            key = ("dma", prod.dma_sem)
            val = prod.dma_val
        else:
            if prod.eng == op.eng and op.eng in ("pe", "sp"):
                return
            key = ("eng", prod.eng)
            val = prod.idx
        if self.waited[op.eng].get(key, -1) >= val:
            return
        self.waited[op.eng][key] = val
        prod.signal = True
        op.waits.append(prod)

    def barrier(self):
        lasts = []
        for e in ENGS:
            if e == "sp":
                continue
            for o in reversed(self.q[e]):
                if o.dma_sem is None:
                    lasts.append(o)
                    break
        dmas = {}
        for e in ENGS:
            for o in self.q[e]:
                if o.dma_sem is not None:
                    dmas[o.dma_sem] = o
        lasts += list(dmas.values())
        for e in ENGS:
            self.pending_barrier[e] = list(lasts)

    def op(self, eng, fn, reads=(), writes=(), dma_sem=None):
        o = Op(eng, fn)
        if self.pending_barrier[eng]:
            for pr in self.pending_barrier[eng]:
                self._need(o, pr)
            self.pending_barrier[eng] = []
        for k in reads:
            r = self.res.get(k)
            if r is not None:
                self._need(o, r[0])
        for k in writes:
            r = self.res.get(k)
            if r is not None:
                self._need(o, r[0])
                for rd in r[1]:
                    self._need(o, rd)
        o.idx = len(self.q[eng])
        self.q[eng].append(o)
        if dma_sem is not None:
            if dma_sem not in self.dma_cnt:
                self.dma_names.append(dma_sem)
            o.dma_sem = dma_sem
            self.dma_cnt[dma_sem] = self.dma_cnt.get(dma_sem, 0) + 16
            o.dma_val = self.dma_cnt[dma_sem]
        for k in reads:
            r = self.res.get(k)
            if r is None:
                self.res[k] = [None, [o]]
            else:
                r[1].append(o)
        for k in writes:
            self.res[k] = [o, []]
        return o

    def wait_only(self, eng, prods):
        o = Op(eng, lambda e: None)
        for pr in prods:
            self._need(o, pr)
        o.idx = len(self.q[eng])
        self.q[eng].append(o)
        return o

    def assign(self):
        for e in ENGS:
            c = 0
            for o in self.q[e]:
                if o.dma_sem is None and o.signal:
                    c += 1
                    o.semval = c

    def emit_engine(self, e, h, sems, dsems):
        for o in self.q[e]:
            for w in o.waits:
                if w.dma_sem is not None:
                    h.wait_ge(dsems[w.dma_sem], w.dma_val)
                else:
                    h.wait_ge(sems[w.eng], w.semval)
            inst = o.fn(h)
            if inst is None:
                continue
            if o.dma_sem is not None:
                inst.then_inc(dsems[o.dma_sem], 16)
            elif o.signal:
                inst.then_inc(sems[e], 1)


def _const_tables():
    slopes = np.exp2(-8.0 * np.arange(1, 9, dtype=np.float64) / 8.0)
    a = np.arange(128)[:, None]
    j = np.arange(640)[None, :]
    dl = a - j + 256
    ad = np.abs(dl)
    mult = (ad <= 64).astype(np.float64) + ((dl % 4 == 0) & (ad <= 256)).astype(np.float64)
    TA = np.empty((8, 128, 640), np.float32)
    for h in range(8):
        with np.errstate(divide="ignore"):
            bb = np.where(mult > 0, np.log(np.maximum(mult, 1e-30)) - slopes[h] * ad, NEG)
        TA[h] = bb.astype(np.float32)
    du = np.abs(np.arange(128)[:, None] - np.arange(128)[None, :])
    B3 = np.empty((8, 128, 128), np.float32)
    for h in range(8):
        B3[h] = np.where(du <= 64, -slopes[h] * 16.0 * du, NEG).astype(np.float32)
    half = 32
    pos = np.arange(T, dtype=np.float32)
    inv = np.power(np.float32(10000.0), -np.arange(half, dtype=np.float32) / half).astype(np.float32)
    ang = (pos[:, None] * inv[None, :]).astype(np.float32)
    cos = np.cos(ang).astype(np.float32).T
    sin = np.sin(ang).astype(np.float32).T
    cosT = np.concatenate([cos, cos], 0)
    sinT = np.concatenate([-sin, sin], 0)
    cosT, sinT = (np.ascontiguousarray(np.concatenate([cosT, cosT], 0)),
                  np.ascontiguousarray(np.concatenate([sinT, sinT], 0)))
    ident = np.eye(128, dtype=np.float32)
    return TA, B3, cosT, sinT, ident


def _prep_shared(w_ada, b_ada, g_pre, w_in, g_q_lora, w_uq, g_kv_lora, w_ukv, w_out, g_post):
    f = np.float32
    w_ada = np.asarray(w_ada, f)[0]
    b_ada = np.asarray(b_ada, f)[0]
    g_pre = np.asarray(g_pre, f)[0]
    w_in = np.asarray(w_in, f)[0]
    g_q = np.asarray(g_q_lora, f)[0]
    w_uq = np.asarray(w_uq, f)[0]
    g_kv = np.asarray(g_kv_lora, f)[0]
    w_ukv = np.asarray(w_ukv, f)[0]
    w_out = np.asarray(w_out, f)[0]
    g_post = np.asarray(g_post, f)[0]
    sh = {}
    sh["w_ada_l"] = np.ascontiguousarray(w_ada.reshape(16, 128, 12, 512).transpose(2, 1, 0, 3)).reshape(12, 128, 8192)
    sh["b_ss"] = np.ascontiguousarray(b_ada[0:4096].reshape(32, 128).T)
    sh["b_gate"] = np.ascontiguousarray(b_ada[4096:6144].reshape(1, 2048))
    sh["g_preT"] = np.ascontiguousarray(g_pre.reshape(16, 128).T)
    perm = []
    for h in range(8):
        for s in range(4):
            perm += list(range(s * 1024 + h * 128, s * 1024 + (h + 1) * 128))
    kpe = list(range(4864, 4928))
    kpe_sw = list(range(4896, 4928)) + list(range(4864, 4896))
    perm += list(range(4096, 4608)) + list(range(4608, 4864)) + kpe + kpe + kpe_sw + kpe_sw
    perm += list(range(4928, 5952))
    perm = np.asarray(perm)
    assert perm.size == NCH * 128
    wp = w_in[:, perm]
    sh["w_in_l"] = np.ascontiguousarray(wp.reshape(16, 128, NCH, 128).transpose(2, 1, 0, 3)).reshape(NCH, 128, 2048)
    pq = []
    for h in range(8):
        b0 = h * 192
        pq += list(range(b0, b0 + 128)) + list(range(b0 + 128, b0 + 192))
        pq += list(range(b0 + 160, b0 + 192)) + list(range(b0 + 128, b0 + 160))
    wq = w_uq[:, np.asarray(pq)]
    sh["w_uq_l"] = np.ascontiguousarray(wq.reshape(4, 128, 8, 256).transpose(2, 1, 0, 3)).reshape(8, 128, 1024)
    sh["w_ukv_l"] = np.ascontiguousarray(w_ukv.reshape(2, 128, 8, 256).transpose(2, 1, 0, 3)).reshape(8, 128, 512)
    sh["w_out_l"] = np.ascontiguousarray(w_out.reshape(16, 128, 2048).transpose(1, 0, 2)).reshape(128, 16 * 2048)
    sh["g_qT"] = np.ascontiguousarray(g_q.reshape(4, 128).T)
    sh["g_kvT"] = np.ascontiguousarray(g_kv.reshape(2, 128).T)
    sh["g_post"] = np.ascontiguousarray(g_post.reshape(1, 2048))
    TA, B3, cosT, sinT, ident = _const_tables()
    sh["TA"] = TA
    sh["B3"] = B3
    sh["cosT"] = cosT
    sh["sinT"] = sinT
    sh["csqT"] = np.ascontiguousarray(np.concatenate([cosT[0:64], sinT[0:64]], 0))
    sh["ident"] = ident
    return sh


_RC = {0: (0, 128), 1: (0, 256), 2: (0, 384), 3: (0, 512), 4: (0, 512), 5: (128, 512), 6: (256, 512), 7: (384, 512)}


def _run_merged(gp, np_, gs, ns):
    done_s = 0
    for i in range(np_):
        next(gp, None)
        target = ((i + 1) * ns) // np_ if np_ else ns
        while done_s < target:
            next(gs, None)
            done_s += 1
    for _ in gp:
        pass
    for _ in gs:
        pass


def build_nc(debug=False):
    nc = bass.Bass("TRN2", target_bir_lowering=False)

    def din(name, shape):
        return nc.dram_tensor(name, list(shape), F32, kind="ExternalInput").ap()

    x_d = din("x", [T, D])
    cT_d = din("cT", [128, 16])
    w_ada_d = din("w_ada_l", [12, 128, 16 * 512])
    b_ss_d = din("b_ss", [128, 32])
    b_gate_d = din("b_gate", [1, 2048])
    g_preT_d = din("g_preT", [128, 16])
    w_in_d = din("w_in_l", [NCH, 128, 16 * 128])
    w_uq_d = din("w_uq_l", [8, 128, 4 * 256])
    w_ukv_d = din("w_ukv_l", [8, 128, 2 * 256])
    w_out_d = din("w_out_l", [128, 16 * 2048])
    g_qT_d = din("g_qT", [128, 4])
    g_kvT_d = din("g_kvT", [128, 2])
    g_post_d = din("g_post", [1, 2048])
    TA_d = din("TA", [8, 128, 640])
    B3_d = din("B3", [8, 128, 128])
    cos_d = din("cosT", [128, T])
    sin_d = din("sinT", [128, T])
    csq_d = din("csqT", [128, T])
    ident_d = din("ident", [128, 128])
    out_d = nc.dram_tensor("out", [T, D], F32, kind="ExternalOutput").ap()
    dbg = {}
    if debug:
        dbg["hT"] = nc.dram_tensor("dbg_hT", [128, 16 * T], BF16, kind="ExternalOutput").ap()
        dbg["cqn"] = nc.dram_tensor("dbg_cqn", [128, 4 * T], BF16, kind="ExternalOutput").ap()
        dbg["ckvn"] = nc.dram_tensor("dbg_ckvn", [128, 2 * T], BF16, kind="ExternalOutput").ap()
        dbg["kpe"] = nc.dram_tensor("dbg_kpe", [64, T], BF16, kind="ExternalOutput").ap()
        dbg["yT"] = nc.dram_tensor("dbg_yT", [128, 16 * T], BF16, kind="ExternalOutput").ap()

    ARW = 52224
    with ExitStack() as es:
        arena = es.enter_context(nc.sbuf_tensor("arena", [128, ARW], F32))
        ps = [es.enter_context(nc.psum_tensor("ps%d" % i, [128, 512], F32)) for i in range(8)]
        A = arena[:]

        def v32(off, n):
            assert off % 4 == 0
            return A[:, off // 4: off // 4 + n]

        def v16(off, n):
            assert off % 4 == 0 and n % 2 == 0
            return A[:, off // 4: off // 4 + n // 2].bitcast(BF16)

        O_HT = 0
        O_YT = 65536
        O_LAT = 131072
        O_CQN = O_LAT
        O_CKVN = O_LAT + 16384
        O_KPE = O_LAT + 24576
        O_TR = 159744
        HT = v16(O_HT, 16 * T).rearrange("p (k t) -> p k t", k=16)
        YT = v16(O_YT, 16 * T).rearrange("p (k t) -> p k t", k=16)
        CQN = v16(O_CQN, 4 * T).rearrange("p (k t) -> p k t", k=4)
        CKVN = v16(O_CKVN, 2 * T).rearrange("p (k t) -> p k t", k=2)
        KPE = v16(O_KPE, T)
        O_ID = O_TR + 44032
        IDENT_BF = v16(O_ID, 128)
        ONES_BF = v16(O_ID + 256, 128)
        O_SM = O_TR + 44544
        SM = v32(O_SM, 256)
        c_sb = SM[:, 0:16]
        sc_f = SM[:, 16:32]
        MS = SM[:, 32:64]
        A_sc = SM[:, 64:80]
        g_pre_sb = SM[:, 80:96]
        b_ss_sb = SM[:, 96:128]
        g_q_sb = SM[:, 128:132]
        g_kv_sb = SM[:, 132:134]
        epsb = SM[:, 134:135]
        ssx = SM[:, 136:152]
        rstdx = SM[:, 152:168]
        rtmp = SM[:, 168:184]
        ssy = SM[:, 184:192]
        ssy1 = SM[:, 192:194]
        rty = SM[:, 194:196]
        rstdy = SM[:, 196:198]
        sc_bf = v16(O_SM + 200 * 4, 16)
        gateT = SM[:, 208:224]
        O_ID32 = O_TR + 45568
        IDENT_F = v32(O_ID32, 128)
        ONES_F = v32(O_ID32 + 512, 128)
        R2 = v32(O_TR + 46592, 512)
        B_sh = MS[:, 0:16]

        p = Prog()
        fin = []

        def ytk(off, n):
            return [("YT", i) for i in range(off // 4096, (off + n - 1) // 4096 + 1)]

        def latk(off, n):
            return [("LAT", i) for i in range(off // 4096, (off + n - 1) // 4096 + 1)]

        def trk(off, n):
            return [("TR", i) for i in range(off // 1024, (off + n - 1) // 1024 + 1)]

        p.op("pool", lambda e: e.dma_start(out=IDENT_BF, in_=ident_d), writes=["identbf"], dma_sem="c0")
        p.op("sp", lambda e: e.dma_start(out=IDENT_F, in_=ident_d), writes=["identf"], dma_sem="c1")
        p.op("sp", lambda e: e.dma_start(out=c_sb, in_=cT_d), writes=["c"], dma_sem="c2")
        p.op("sp", lambda e: e.dma_start(out=g_pre_sb, in_=g_preT_d), writes=["gpre"], dma_sem="c3")
        p.op("sp", lambda e: e.dma_start(out=b_ss_sb, in_=b_ss_d), writes=["bss"], dma_sem="c4")
        p.op("sp", lambda e: e.dma_start(out=g_q_sb, in_=g_qT_d), writes=["gq"], dma_sem="c5")
        p.op("sp", lambda e: e.dma_start(out=g_kv_sb, in_=g_kvT_d), writes=["gkv"], dma_sem="c6")
        p.op("dve", lambda e: e.memset(ONES_BF, 1.0), writes=["onesbf"])
        p.op("dve", lambda e: e.memset(ONES_F, 1.0), writes=["onesf"])
        p.op("dve", lambda e: e.memset(epsb, EPS), writes=["eps"])
        one_col = ONES_F[:, 0:1]
        p.op("act", lambda e: e.activation(out=sc_f, in_=c_sb, func=AF.Exp, scale=-1.0), reads=["c"], writes=["scf"])
        p.op("act", lambda e: e.activation(out=sc_f, in_=sc_f, func=AF.Ln, bias=one_col), reads=["scf", "onesf"], writes=["scf"])
        p.op("act", lambda e: e.activation(out=sc_f, in_=sc_f, func=AF.Exp, scale=-1.0), reads=["scf"], writes=["scf"])
        p.op("dve", lambda e: e.tensor_tensor(out=sc_f, in0=sc_f, in1=c_sb, op=ALU.mult), reads=["scf", "c"], writes=["scf"])
        p.op("dve", lambda e: e.tensor_copy(out=sc_bf, in_=sc_f), reads=["scf"], writes=["scbf"])

        WA = [v16(O_YT + s * 16384, 16 * 512).rearrange("p (k n) -> p k n", k=16) for s in range(2)]
        WAK = [ytk(s * 16384, 16384) for s in range(2)]
        NXT = 4
        XT = [v32(O_YT + 32768 + s * 8192, 2048) for s in range(NXT)]
        XTK = [ytk(32768 + s * 8192, 8192) for s in range(NXT)]
        JUNK = v16(O_LAT + 16384, 2048)
        JUNKK = latk(16384, 4096)
        DG = [v32(O_LAT + 20480 + s * 512, 128) for s in range(2)]

        def x_front(tb):
            s = tb % NXT
            s2 = tb % 2
            p.op("sp", lambda e: e.dma_start(out=XT[s], in_=x_d[tb * 128:(tb + 1) * 128, :]),
                 writes=XTK[s], dma_sem="xt%d" % s)
            p.op("act", lambda e: e.activation(out=JUNK, in_=XT[s], func=AF.Square, accum_out=ssx[:, tb:tb + 1]),
                 reads=XTK[s], writes=JUNKK + [("ssx", tb)])
            p.op("act", lambda e: e.activation(out=rtmp[:, tb:tb + 1], in_=ssx[:, tb:tb + 1], func=AF.Ln,
                                               bias=epsb, scale=1.0 / D),
                 reads=[("ssx", tb), "eps"], writes=[("rtmp", tb)])
            p.op("act", lambda e: e.activation(out=rstdx[:, tb:tb + 1], in_=rtmp[:, tb:tb + 1], func=AF.Exp, scale=-0.5),
                 reads=[("rtmp", tb)], writes=[("rstdx", tb)])
            p.op("dve", lambda e: e.tensor_scalar(out=DG[s2], in0=IDENT_F, scalar1=rstdx[:, tb:tb + 1], scalar2=1.0,
                                                  op0=ALU.mult, op1=ALU.mult),
                 reads=["identf", ("rstdx", tb)], writes=[("DG", s2)])
            for kc in range(16):
                b = 4 * s2 + kc // 4
                p.op("pe", lambda e, kc=kc, b=b: e.matmul(
                    ps[b][:, (kc % 4) * 128:(kc % 4 + 1) * 128], lhsT=XT[s][:, kc * 128:(kc + 1) * 128], rhs=DG[s2],
                    start=True, stop=True),
                    reads=XTK[s] + [("DG", s2)], writes=[("ps", b)])

        def x_back(tb):
            s = tb % 2
            for kc in range(16):
                b = 4 * s + kc // 4
                src = ps[b][:, (kc % 4) * 128:(kc % 4 + 1) * 128]
                dst = HT[:, kc, tb * 128:(tb + 1) * 128]
                if b % 4 != 3:
                    p.op("dve", lambda e, src=src, dst=dst, kc=kc: e.scalar_tensor_tensor(
                        out=dst, in0=src, scalar=A_sc[:, kc:kc + 1], in1=B_sh[:, kc:kc + 1].to_broadcast([128, 128]),
                        op0=ALU.mult, op1=ALU.add),
                        reads=[("ps", b), "Asc", "MS"], writes=[("HT", tb // 4, kc)])
                else:
                    p.op("act", lambda e, src=src, dst=dst, kc=kc: e.activation(
                        out=dst, in_=src, func=AF.Identity, bias=B_sh[:, kc:kc + 1], scale=A_sc[:, kc:kc + 1]),
                        reads=[("ps", b), "Asc", "MS"], writes=[("HT", tb // 4, kc)])

        psM = ps[7]
        for g in range(8):
            s = g % 2
            p.op("pool", lambda e, g=g, s=s: e.dma_start(out=WA[s].rearrange("p k n -> p (k n)"), in_=w_ada_d[g]),
                 writes=WAK[s], dma_sem="wa%d" % s)
            for sub in range(4):
                jn = 4 * g + sub
                for kc in range(16):
                    p.op("pe", lambda e, s=s, sub=sub, jn=jn, kc=kc: e.matmul(
                        psM[:, jn:jn + 1], lhsT=WA[s][:, kc, sub * 128:(sub + 1) * 128], rhs=sc_bf[:, kc:kc + 1],
                        start=(kc == 0), stop=(kc == 15)),
                        reads=WAK[s] + ["scbf"], writes=[("ps", 7)])
            if g == 5:
                x_front(0)
        p.op("dve", lambda e: e.tensor_tensor(out=MS, in0=psM[:, 0:32], in1=b_ss_sb, op=ALU.add),
             reads=[("ps", 7), "bss"], writes=["MS"])
        p.op("dve", lambda e: e.scalar_tensor_tensor(out=A_sc, in0=MS[:, 16:32], scalar=1.0, in1=g_pre_sb,
                                                      op0=ALU.add, op1=ALU.mult),
             reads=["MS", "gpre"], writes=["Asc"])
        for tb in range(16):
            if tb + 1 < 16:
                x_front(tb + 1)
            x_back(tb)
        HTK = [("HT", tc, kc) for tc in range(4) for kc in range(16)]

        NWIN = 2
        WIN = [v16(O_TR + i * 4096, 16 * 128).rearrange("p (k n) -> p k n", k=16) for i in range(NWIN)]
        WINK = [trk(i * 4096, 4096) for i in range(NWIN)]
        FA = v32(O_TR + 8192, 512)
        FB = v32(O_TR + 10240, 512)
        TA_sb = [v32(O_TR + 12288, 640)] * 2
        TAK = [["TA0"]] * 2
        B3_sb = [v32(O_TR + 14848, 128)] * 2
        B3K = [["B30"]] * 2
        TZ = v32(O_TR + 15360, 512)
        NPT = 5
        PT = [v16(O_TR + 18432 + i * 1024, 512) for i in range(NPT)]
        PTK = [["PT%d" % i] for i in range(NPT)]
        NTMP = 2
        TMP = [v32(O_TR + 23552 + i * 2048, 512) for i in range(NTMP)]
        TMPK = [["TMP%d" % i] for i in range(NTMP)]
        WG = v16(O_YT + 57344, 16 * 256).rearrange("p (k n) -> p k n", k=16)
        WGK = [("YT", 14), ("YT", 15)]
        U3S = v32(O_TR + 27648, 2048)
        L3S = v32(O_TR + 35840, 2048)
        U3SK = trk(27648, 8192)
        L3SK = trk(35840, 8192)

        def mkset(base, keyfn):
            d = {}
            names = ["QT", "KT", "VT", "VN", "V16", "GA"]
            for i, nm in enumerate(names):
                d[nm] = v16(base + i * 4096, T)
                d[nm + "K"] = keyfn(i)
            d["P3T"] = d["VT"].rearrange("p (r u) -> p r u", r=16)
            d["VN"] = d["VN"].rearrange("p (b d) -> p b d", b=16)
            d["V16"] = d["V16"].rearrange("p (r d) -> p r d", r=16)
            return d
        SETS = [mkset(O_LAT, lambda i: [("LAT", i)]), mkset(O_YT + 32768, lambda i: [("YT", 8 + i)])]

        tile_ctr = [0]
        ORDER = list(range(32)) + list(range(40, 48))
        pos_of = {j: i for i, j in enumerate(ORDER)}
        loaded = [0]

        def ensure_loaded(upto):
            while loaded[0] <= upto and loaded[0] < len(ORDER):
                pos = loaded[0]
                j = ORDER[pos]
                s = pos % NWIN
                p.op("pool", lambda e, s=s, j=j: e.dma_start(out=WIN[s].rearrange("p k n -> p (k n)"), in_=w_in_d[j]),
                     writes=WINK[s], dma_sem="win%d" % s)
                loaded[0] += 1

        nbanks = [2]

        def proj_tile(wt, wk, ncols, rhs_fn, nk, rkeys, m0=0, m1=128):
            b = tile_ctr[0] % nbanks[0]
            tile_ctr[0] += 1
            for kc in range(nk):
                p.op("pe", lambda e, kc=kc, b=b: e.matmul(
                    ps[b][0:(m1 - m0), 0:ncols], lhsT=wt[:, kc, m0:m1], rhs=rhs_fn(kc),
                    start=(kc == 0), stop=(kc == nk - 1)),
                    reads=wk + (rkeys(kc) if callable(rkeys) else rkeys), writes=[("ps", b)])
            return b

        win_slot = {}

        def ht_tile(j, tc, m0=0, m1=128):
            s = win_slot[j] if j in win_slot else pos_of[j] % NWIN
            return proj_tile(WIN[s], WINK[s], 512, lambda kc: HT[:, kc, tc * 512:(tc + 1) * 512], 16,
                             lambda kc: [("HT", tc, kc)], m0, m1)

        def silu_evac(b, dst, dstk):
            p.op("act", lambda e: e.activation(out=TZ, in_=ps[b][:, :], func=AF.Exp, scale=-1.0),
                 reads=[("ps", b)], writes=["TZ"])
            p.op("act", lambda e: e.activation(out=TZ, in_=TZ, func=AF.Ln, bias=one_col), reads=["TZ", "onesf"], writes=["TZ"])
            p.op("act", lambda e: e.activation(out=TZ, in_=TZ, func=AF.Exp, scale=-1.0), reads=["TZ"], writes=["TZ"])
            p.op("dve", lambda e: e.tensor_tensor(out=dst, in0=ps[b][:, :], in1=TZ, op=ALU.mult),
                 reads=[("ps", b), "TZ"], writes=dstk)

        def gen_inproj_head(h):
            S = SETS[(h + 1) % 2]
            hs = h % 2
            for sidx in range(4):
                j = 4 * h + sidx
                ensure_loaded(pos_of[j] + 1)
                for tc in range(4):
                    sl = slice(tc * 512, (tc + 1) * 512)
                    b = ht_tile(j, tc)
                    if sidx == 0:
                        p.op("dve", lambda e, b=b, sl=sl: e.tensor_copy(out=S["QT"][:, sl], in_=ps[b][:, :]),
                             reads=[("ps", b)], writes=S["QTK"])
                    elif sidx == 1:
                        p.op("act", lambda e, b=b, sl=sl: e.activation(out=S["KT"][:, sl], in_=ps[b][:, :], func=AF.Copy),
                             reads=[("ps", b)], writes=S["KTK"])
                    elif sidx == 2:
                        p.op("dve", lambda e, b=b, sl=sl: e.tensor_copy(out=S["VT"][:, sl], in_=ps[b][:, :]),
                             reads=[("ps", b)], writes=S["VTK"])
                    else:
                        silu_evac(b, S["GA"][:, sl], S["GAK"])
                    yield

        def gen_gate():
            def dma(g8):
                src = w_ada_d[8 + g8 // 2].rearrange("p (k n) -> p k n", k=16)[:, :, (g8 % 2) * 256:(g8 % 2 + 1) * 256]
                p.op("pool", lambda e, src=src: e.dma_start(out=WG, in_=src), writes=WGK, dma_sem="wg")

            def mm(g8):
                b = tile_ctr[0] % 2
                tile_ctr[0] += 1
                for sub in range(2):
                    for kc in range(16):
                        p.op("pe", lambda e, sub=sub, kc=kc, b=b: e.matmul(
                            ps[b][:, sub:sub + 1], lhsT=WG[:, kc, sub * 128:(sub + 1) * 128], rhs=sc_bf[:, kc:kc + 1],
                            start=(kc == 0), stop=(kc == 15)),
                            reads=WGK + ["scbf"], writes=[("ps", b)])
                p.op("dve", lambda e, b=b, g8=g8: e.tensor_copy(out=gateT[:, 2 * g8:2 * g8 + 2], in_=ps[b][:, 0:2]),
                     reads=[("ps", b)], writes=[("gateT", g8)])

            dma(0)
            yield
            for g8 in range(1, 8):
                mm(g8 - 1)
                dma(g8)
                yield
            mm(7)
            yield

        def gen_attn(h):
            S = SETS[(h + 1) % 2]
            hs = h % 2
            p.op("sp", lambda e: e.dma_start(out=TA_sb[hs], in_=TA_d[h]), writes=TAK[hs], dma_sem="ta0")
            p.op("sp", lambda e: e.dma_start(out=B3_sb[hs], in_=B3_d[h]), writes=B3K[hs], dma_sem="b30")
            QT, KT, VT, P3T, VN, V16, GA = S["QT"], S["KT"], S["VT"], S["P3T"], S["VN"], S["V16"], S["GA"]
            QTK, KTK, VTK, VNK, V16K, GAK = S["QTK"], S["KTK"], S["VTK"], S["VNK"], S["V16K"], S["GAK"]
            cnt = [0]
            for which in range(2):
                for g in range(4):
                    sb = 2 + cnt[0] % 6
                    cnt[0] += 1
                    for bi in range(4):
                        idx = 4 * g + bi
                        src = VT[:, idx * 128:(idx + 1) * 128] if which == 0 else VT[:, idx:T:16]
                        p.op("pe", lambda e, sb=sb, bi=bi, src=src: e.matmul(ps[sb][:, bi * 128:(bi + 1) * 128], lhsT=src,
                                                                          rhs=IDENT_BF, start=True, stop=True),
                             reads=VTK + ["identbf"], writes=[("ps", sb)])
                    dst = (VN if which == 0 else V16)[:, 4 * g:4 * g + 4, :]
                    dk = VNK if which == 0 else V16K
                    srcp = ps[sb][:, :].rearrange("p (b d) -> p b d", b=4)
                    if g % 2 == 0:
                        p.op("dve", lambda e, dst=dst, srcp=srcp: e.tensor_copy(out=dst, in_=srcp),
                             reads=[("ps", sb)], writes=dk)
                    else:
                        p.op("act", lambda e, dst=dst, srcp=srcp: e.activation(out=dst, in_=srcp, func=AF.Copy),
                             reads=[("ps", sb)], writes=dk)
                    yield
            for g in range(4):
                sb = 2 + cnt[0] % 6
                ti = cnt[0] % NTMP
                cnt[0] += 1
                for ri in range(4):
                    r = 4 * g + ri
                    p.op("pe", lambda e, sb=sb, ri=ri, r=r: e.matmul(ps[sb][:, ri * 128:(ri + 1) * 128],
                                                                    lhsT=KT[:, r:T:16], rhs=QT[:, r:T:16],
                                                                    start=True, stop=True),
                         reads=KTK + QTK, writes=[("ps", sb)])
                p.op("dve", lambda e, sb=sb, ti=ti: e.scalar_tensor_tensor(
                    out=TMP[ti].rearrange("p (r u) -> p r u", r=4), in0=ps[sb][:, :].rearrange("p (r u) -> p r u", r=4),
                    scalar=SC_A, in1=B3_sb[hs].unsqueeze(1).to_broadcast([128, 4, 128]), op0=ALU.mult, op1=ALU.add),
                    reads=[("ps", sb)] + B3K[hs], writes=TMPK[ti])
                p.op("act", lambda e, g=g, ti=ti: e.activation(out=P3T[:, 4 * g:4 * g + 4, :],
                                                              in_=TMP[ti].rearrange("p (r u) -> p r u", r=4), func=AF.Exp),
                     reads=TMPK[ti], writes=VTK)
                yield
            U3v = U3S.rearrange("p (u r) -> p r u", r=16)
            L3v = L3S.rearrange("p (u r) -> p r u", r=16)
            for g in range(4):
                sbu = 2 + cnt[0] % 6
                sbl = 2 + (cnt[0] + 1) % 6
                cnt[0] += 2
                for ri in range(4):
                    r = 4 * g + ri
                    p.op("pe", lambda e, sbu=sbu, ri=ri, r=r: e.matmul(ps[sbu][:, ri * 128:(ri + 1) * 128],
                                                                      lhsT=V16[:, r, :], rhs=P3T[:, r, :],
                                                                      start=True, stop=True),
                         reads=V16K + VTK, writes=[("ps", sbu)])
                for ri in range(4):
                    r = 4 * g + ri
                    p.op("pe", lambda e, sbl=sbl, ri=ri, r=r: e.matmul(ps[sbl][:, ri * 128:(ri + 1) * 128],
                                                                      lhsT=ONES_BF, rhs=P3T[:, r, :],
                                                                      start=True, stop=True),
                         reads=["onesbf"] + VTK, writes=[("ps", sbl)])
                p.op("dve", lambda e, sbu=sbu, g=g: e.tensor_copy(
                    out=U3v[:, 4 * g:4 * g + 4, :], in_=ps[sbu][:, :].rearrange("p (r u) -> p r u", r=4)),
                    reads=[("ps", sbu)], writes=U3SK)
                p.op("act", lambda e, sbl=sbl, g=g: e.activation(
                    out=L3v[:, 4 * g:4 * g + 4, :], in_=ps[sbl][:, :].rearrange("p (r u) -> p r u", r=4), func=AF.Copy),
                    reads=[("ps", sbl)], writes=L3SK)
                yield
            defer = []

            def flush(keep=0):
                while len(defer) > keep:
                    defer.pop(0)()

            for qc in range(4):
                order = [c for c in (3, 4, 2, 5, 1, 6, 0, 7) if 0 <= 4 * qc - 2 + c <= 15]
                for si, c in enumerate(order):
                    kb = 4 * qc - 2 + c
                    i0, i1 = _RC[c]
                    n = i1 - i0
                    j0 = i0 + 512 - 128 * c
                    sb = 2 + cnt[0] % 4
                    pi = cnt[0] % NPT
                    ti = cnt[0] % NTMP
                    cnt[0] += 1
                    p.op("pe", lambda e, sb=sb, kb=kb, i0=i0, i1=i1, qc=qc: e.matmul(
                        ps[sb][:, i0:i1], lhsT=KT[:, kb * 128:(kb + 1) * 128], rhs=QT[:, qc * 512 + i0:qc * 512 + i1],
                        start=True, stop=True),
                        reads=KTK + QTK, writes=[("ps", sb)])
                    flush(4)
                    p.op("dve", lambda e, sb=sb, ti=ti, i0=i0, i1=i1, j0=j0, n=n: e.scalar_tensor_tensor(
                        out=TMP[ti][:, i0:i1], in0=ps[sb][:, i0:i1], scalar=SC_A, in1=TA_sb[hs][:, j0:j0 + n],
                        op0=ALU.mult, op1=ALU.add),
                        reads=[("ps", sb)] + TAK[hs], writes=TMPK[ti])
                    p.op("act", lambda e, pi=pi, ti=ti, i0=i0, i1=i1: e.activation(out=PT[pi][:, i0:i1],
                                                                                  in_=TMP[ti][:, i0:i1], func=AF.Exp),
                         reads=TMPK[ti], writes=PTK[pi])

                    def pv(kb=kb, pi=pi, i0=i0, i1=i1, first=(si == 0), last=(si == len(order) - 1), bu=6):
                        p.op("pe", lambda e: e.matmul(ps[bu][:, i0:i1], lhsT=VN[:, kb, :], rhs=PT[pi][:, i0:i1],
                                                      start=first, stop=last, skip_group_check=True),
                             reads=VNK + PTK[pi], writes=[("ps", bu)])
                        p.op("pe", lambda e: e.matmul(ps[bu + 1][:, i0:i1], lhsT=ONES_BF, rhs=PT[pi][:, i0:i1],
                                                      start=first, stop=last, skip_group_check=True),
                             reads=["onesbf"] + PTK[pi], writes=[("ps", bu + 1)])
                    defer.append(pv)
                    yield
                flush(0)
                qs = slice(qc * 512, (qc + 1) * 512)
                bu = 6
                p.op("dve", lambda e, qs=qs, bu=bu: e.tensor_tensor(out=FA, in0=ps[bu + 1][:, :], in1=L3S[:, qs], op=ALU.add),
                     reads=[("ps", bu + 1)] + L3SK, writes=["FA"])
                p.op("dve", lambda e, qs=qs, bu=bu: e.tensor_tensor(out=FB, in0=ps[bu][:, :], in1=U3S[:, qs], op=ALU.add),
                     reads=[("ps", bu)] + U3SK, writes=["FB"])
                p.op("act", lambda e: e.activation(out=FA, in_=FA, func=AF.Ln), reads=["FA"], writes=["FA"])
                p.op("act", lambda e: e.activation(out=FA, in_=FA, func=AF.Exp, scale=-1.0), reads=["FA"], writes=["FA"])
                p.op("pool", lambda e: e.tensor_tensor(out=FB, in0=FB, in1=FA, op=ALU.mult),
                     reads=["FA", "FB"], writes=["FB"])
                p.op("pool", lambda e, qs=qs: e.tensor_tensor(out=YT[:, h, qs], in0=FB, in1=GA[:, qs], op=ALU.mult),
                     reads=["FB"] + GAK, writes=[("YT", h)])
                yield

        N_ATT = 8 + 4 + 4 + 4 * 1 + sum(len([c for c in range(8) if 0 <= 4 * qc - 2 + c <= 15]) for qc in range(4))

        def gen_bz():
            for h in range(8):
                j = 40 + h
                ensure_loaded(pos_of[j] + 1)
                for tc in range(4):
                    b = ht_tile(j, tc)
                    silu_evac(b, YT[:, 8 + h, tc * 512:(tc + 1) * 512], [("YT", 8 + h)])
                    yield

        ensure_loaded(1)
        for _ in gen_inproj_head(0):
            pass
        gate_gen = gen_gate()

        def attn_with_gate(h, every):
            for i, _ in enumerate(gen_attn(h)):
                if i % every == every - 1:
                    next(gate_gen, None)
                yield

        for h in range(8):
            if h < 7:
                sec = attn_with_gate(h, N_ATT // 5) if h in (1, 2) else gen_attn(h)
                _run_merged(gen_inproj_head(h + 1), 16, sec, N_ATT)
            else:
                _run_merged(gen_bz(), 32, gen_attn(h), N_ATT)
        for _ in gate_gen:
            pass

        p.barrier()
        O_LT = O_TR + 8192
        CQRAW = [v32(O_LT + j * 8192, 2048) for j in range(4)]
        defer = []

        def flush():
            while defer:
                defer.pop(0)()

        def evac_latent(tc, b, raw, rawk, SQl, SQK, first, last, ctr):
            sl = slice(tc * 512, (tc + 1) * 512)
            p.op("act", lambda e: e.activation(out=raw[:, sl], in_=ps[b][:, :], func=AF.Copy),
                 reads=[("ps", b)], writes=rawk)
            i = ctr[0] % 2
            ctr[0] += 1
            p.op("act", lambda e: e.activation(out=SQl[i], in_=ps[b][:, :], func=AF.Square),
                 reads=[("ps", b)], writes=SQK[i])
            defer.append(lambda: p.op("pe", lambda e: e.matmul(
                ps[2 + tc][:, :], lhsT=ONES_F, rhs=SQl[i], start=first, stop=last),
                reads=SQK[i] + ["onesf"], writes=[("ps", 2 + tc)]))

        def norm_latent(raws, rawks, nch, gsb, dst, dstk, nfeat, RSl, RSK):
            for tc in range(4):
                i = tc % 2
                p.op("act", lambda e, tc=tc, i=i: e.activation(out=RSl[i], in_=ps[2 + tc][:, :], func=AF.Ln,
                                                               bias=epsb, scale=1.0 / nfeat),
                     reads=[("ps", 2 + tc), "eps"], writes=RSK[i])
                p.op("act", lambda e, i=i: e.activation(out=RSl[i], in_=RSl[i], func=AF.Exp, scale=-0.5),
                     reads=RSK[i], writes=RSK[i])
                for jj in range(nch):
                    p.op("dve", lambda e, jj=jj, tc=tc, i=i: e.scalar_tensor_tensor(
                        out=dst[:, jj, tc * 512:(tc + 1) * 512], in0=raws[jj][:, tc * 512:(tc + 1) * 512],
                        scalar=gsb[:, jj:jj + 1], in1=RSl[i], op0=ALU.mult, op1=ALU.mult),
                        reads=rawks[jj] + RSK[i] + ["gq", "gkv"], writes=dstk(jj))

        lat_loaded = [32]

        def load_lat(j):
            while lat_loaded[0] <= j and lat_loaded[0] <= 39:
                jj_ = lat_loaded[0]
                s = jj_ % 2
                win_slot[jj_] = s
                p.op("pool", lambda e, s=s, jj_=jj_: e.dma_start(out=WIN[s].rearrange("p k n -> p (k n)"), in_=w_in_d[jj_]),
                     writes=WINK[s], dma_sem="win%d" % s)
                lat_loaded[0] += 1

        load_lat(33)
        SQa = [v32(O_CKVN + i * 2048, 512) for i in range(2)]
        RSa = [v32(O_CKVN + 4096 + i * 2048, 512) for i in range(2)]
        SQaK = [["SQa%d" % i] for i in range(2)]
        RSaK = [["RSa%d" % i] for i in range(2)]
        CQRAWK = [["CQRAW%d" % j] for j in range(4)]
        ctr = [0]
        for jj in range(4):
            j = 32 + jj
            load_lat(j + 1)
            for tc in range(4):
                b = ht_tile(j, tc)
                flush()
                evac_latent(tc, b, CQRAW[jj], CQRAWK[jj], SQa, SQaK, jj == 0, jj == 3, ctr)
        flush()
        norm_latent(CQRAW, CQRAWK, 4, g_q_sb, CQN, lambda jj: [("LAT", jj)], 512, RSa, RSaK)
        p.barrier()
        CKRAW = [v32(O_LT + j * 8192, 2048) for j in range(2)]
        CKRAWK = [["CKRAW%d" % j] for j in range(2)]
        SQb = [v32(O_LT + 16384 + i * 2048, 512) for i in range(2)]
        RSb = [v32(O_LT + 20480 + i * 2048, 512) for i in range(2)]
        SQbK = [["SQb%d" % i] for i in range(2)]
        RSbK = [["RSb%d" % i] for i in range(2)]
        for jj in range(2):
            j = 36 + jj
            load_lat(j + 1)
            for tc in range(4):
                b = ht_tile(j, tc)
                flush()
                evac_latent(tc, b, CKRAW[jj], CKRAWK[jj], SQb, SQbK, jj == 0, jj == 1, ctr)
        flush()
        norm_latent(CKRAW, CKRAWK, 2, g_kv_sb, CKVN, lambda jj: [("LAT", 4 + jj)], 256, RSb, RSbK)
        p.barrier()
        COS1 = v32(O_LT, 2048)
        SIN1 = v32(O_LT + 8192, 2048)
        T1 = v32(O_LT + 16384, 512)
        T2 = v32(O_LT + 18432, 512)
        p.op("sp", lambda e: e.dma_start(out=COS1, in_=cos_d), writes=["COS1"], dma_sem="cos")
        p.op("sp", lambda e: e.dma_start(out=SIN1, in_=sin_d), writes=["SIN1"], dma_sem="sin")
        load_lat(39)
        for tc in range(4):
            sl = slice(tc * 512, (tc + 1) * 512)
            ba = ht_tile(38, tc)
            bb = ht_tile(39, tc)
            p.op("dve", lambda e, ba=ba, sl=sl: e.tensor_tensor(out=T1, in0=ps[ba][:, :], in1=COS1[:, sl], op=ALU.mult),
                 reads=[("ps", ba), "COS1"], writes=["T1"])
            p.op("dve", lambda e, bb=bb, sl=sl: e.tensor_tensor(out=T2, in0=ps[bb][:, :], in1=SIN1[:, sl], op=ALU.mult),
                 reads=[("ps", bb), "SIN1"], writes=["T2"])
            p.op("dve", lambda e, sl=sl: e.tensor_tensor(out=KPE[:, sl], in0=T1, in1=T2, op=ALU.add),
                 reads=["T1", "T2"], writes=[("LAT", 6)])

        if debug:
            fin.append(p.op("sp", lambda e: e.dma_start(out=dbg["hT"], in_=HT.rearrange("p k t -> p (k t)")),
                            reads=HTK, dma_sem="dbg0"))
            fin.append(p.op("sp", lambda e: e.dma_start(out=dbg["cqn"], in_=CQN.rearrange("p k t -> p (k t)")),
                            reads=[("LAT", i) for i in range(4)], dma_sem="dbg1"))
            fin.append(p.op("sp", lambda e: e.dma_start(out=dbg["ckvn"], in_=CKVN.rearrange("p k t -> p (k t)")),
                            reads=[("LAT", 4), ("LAT", 5)], dma_sem="dbg2"))
            fin.append(p.op("sp", lambda e: e.dma_start(out=dbg["kpe"], in_=KPE[0:64, :]), reads=[("LAT", 6)],
                            dma_sem="dbg3"))

        p.barrier()
        WOUT = v16(O_HT, 16 * 2048).rearrange("p (k n) -> p k n", k=16)
        WUQ = v16(O_TR + 0, 4 * 256).rearrange("p (k n) -> p k n", k=4)
        WUKV = v16(O_TR + 2048, 2 * 256).rearrange("p (k n) -> p k n", k=2)
        QN = v16(O_TR + 4096, T)
        KN = v16(O_TR + 8192, T)
        QPE = v16(O_TR + 12288, T)
        VB = v16(O_TR + 16384, T).rearrange("p (b d) -> p b d", b=16)
        PT2 = [v16(O_TR + 20480 + i * 1024, 512) for i in range(6)]
        R1 = v32(O_TR + 26624, 512)
        CSQ = v32(O_HT + 49152, 2048)
        CQNK = [("LAT", i) for i in range(4)]
        CKVNK = [("LAT", 4), ("LAT", 5)]
        KPEK = [("LAT", 6)]

        def load_wb(h):
            p.op("pool", lambda e: e.dma_start(out=WUQ.rearrange("p k n -> p (k n)"), in_=w_uq_d[h]),
                 writes=["WUQ"], dma_sem="wuq")
            p.op("pool", lambda e: e.dma_start(out=WUKV.rearrange("p k n -> p (k n)"), in_=w_ukv_d[h]),
                 writes=["WUKV"], dma_sem="wukv")

        load_wb(0)
        p.op("sp", lambda e: e.dma_start(out=CSQ, in_=csq_d), writes=[("WOUT", 3)], dma_sem="cos")

        def load_wout(g):
            p.op("pool", lambda e: e.dma_start(
                out=WOUT[:, 4 * g:4 * g + 4, :].rearrange("p k n -> p (k n)"),
                in_=w_out_d[:, g * 8192:(g + 1) * 8192]),
                writes=[("WOUT", g)], dma_sem="wout%d" % g)

        nbanks[0] = 4
        for h in range(8):
            for tc in range(4):
                sl = slice(tc * 512, (tc + 1) * 512)
                b = proj_tile(WUKV, ["WUKV"], 512, lambda kc, sl=sl: CKVN[:, kc, sl], 2, CKVNK, 0, 128)
                p.op("act", lambda e, b=b, sl=sl: e.activation(out=KN[:, sl], in_=ps[b][:, :], func=AF.Copy),
                     reads=[("ps", b)], writes=["KN"])
            for tc in range(4):
                sl = slice(tc * 512, (tc + 1) * 512)
                b = proj_tile(WUQ, ["WUQ"], 512, lambda kc, sl=sl: CQN[:, kc, sl], 4, CQNK, 0, 128)
                p.op("dve", lambda e, b=b, sl=sl: e.tensor_copy(out=QN[:, sl], in_=ps[b][:, :]),
                     reads=[("ps", b)], writes=["QN"])
                ba = proj_tile(WUQ, ["WUQ"], 512, lambda kc, sl=sl: CQN[:, kc, sl], 4, CQNK, 128, 256)
                p.op("dve", lambda e, ba=ba, sl=sl: e.tensor_tensor(out=QPE[:, sl], in0=ps[ba][:, :], in1=CSQ[:, sl],
                                                                    op=ALU.mult),
                     reads=[("ps", ba), ("WOUT", 3)], writes=["QPE"])
            for g in range(4):
                b = tile_ctr[0] % nbanks[0]
                tile_ctr[0] += 1
                for bi in range(4):
                    blk = 4 * g + bi
                    for kc in range(2):
                        p.op("pe", lambda e, b=b, bi=bi, blk=blk, kc=kc: e.matmul(
                            ps[b][:, bi * 128:(bi + 1) * 128], lhsT=CKVN[:, kc, blk * 128:(blk + 1) * 128],
                            rhs=WUKV[:, kc, 128:256], start=(kc == 0), stop=(kc == 1)),
                            reads=CKVNK + ["WUKV"], writes=[("ps", b)])
                p.op("dve", lambda e, b=b, g=g: e.tensor_copy(
                    out=VB[:, 4 * g:4 * g + 4, :], in_=ps[b][:, :].rearrange("p (b d) -> p b d", b=4)),
                    reads=[("ps", b)], writes=["VB"])
            if h < 7:
                load_wb(h + 1)
            if h < 3:
                load_wout(h)
            if h == 7:
                load_wout(3)
            defer = []

            def flush2(keep=0):
                while len(defer) > keep:
                    defer.pop(0)()

            cnt = 0
            for qc in range(4):
                qs = slice(qc * 512, (qc + 1) * 512)
                for kb in range(16):
                    sb = 2 + cnt % 2
                    pi = cnt % 6
                    cnt += 1
                    ks = slice(kb * 128, (kb + 1) * 128)
                    p.op("pe", lambda e, sb=sb, ks=ks, qs=qs: e.matmul(ps[sb][:, :], lhsT=KN[:, ks], rhs=QN[:, qs],
                                                                      start=True, stop=False),
                         reads=["KN", "QN"], writes=[("ps", sb)])
                    p.op("pe", lambda e, sb=sb, ks=ks, qs=qs: e.matmul(ps[sb][:, :], lhsT=KPE[:, ks], rhs=QPE[:, qs],
                                                                      start=False, stop=True),
                         reads=KPEK + ["QPE"], writes=[("ps", sb)])
                    flush2(3)
                    p.op("act", lambda e, sb=sb, pi=pi: e.activation(out=PT2[pi], in_=ps[sb][:, :], func=AF.Exp,
                                                                    scale=SC_B),
                         reads=[("ps", sb)], writes=[("PT2", pi)])

                    def pv(kb=kb, pi=pi, bu=4 + 2 * (qc % 2)):
                        p.op("pe", lambda e: e.matmul(ps[bu][:, :], lhsT=VB[:, kb, :], rhs=PT2[pi],
                                                      start=(kb == 0), stop=(kb == 15)),
                             reads=["VB", ("PT2", pi)], writes=[("ps", bu)])
                        p.op("pe", lambda e: e.matmul(ps[bu + 1][:, :], lhsT=ONES_BF, rhs=PT2[pi],
                                                      start=(kb == 0), stop=(kb == 15)),
                             reads=["onesbf", ("PT2", pi)], writes=[("ps", bu + 1)])
                    defer.append(pv)
                yv = YT[:, 8 + h, qs]
                bu = 4 + 2 * (qc % 2)

                def fin_qc(yv=yv, bu=bu):
                    p.op("act", lambda e: e.activation(out=R1, in_=ps[bu + 1][:, :], func=AF.Ln),
                         reads=[("ps", bu + 1)], writes=["R1"])
                    p.op("act", lambda e: e.activation(out=R1, in_=R1, func=AF.Exp, scale=-1.0), reads=["R1"], writes=["R1"])
                    p.op("dve", lambda e: e.tensor_tensor(out=R2, in0=ps[bu][:, :], in1=R1, op=ALU.mult),
                         reads=[("ps", bu), "R1"], writes=["R2"])
                    p.op("pool", lambda e: e.tensor_tensor(out=yv, in0=R2, in1=yv, op=ALU.mult),
                         reads=["R2", ("YT", 8 + h)], writes=[("YT", 8 + h)])
                defer.append(fin_qc)
            flush2(0)
            if False:
                p.op("dve", lambda e: e.reciprocal(out=R1, in_=ps[7][:, :]), reads=[("ps", 7)], writes=["R1"])
                p.op("dve", lambda e: e.tensor_tensor(out=R2, in0=ps[6][:, :], in1=R1, op=ALU.mult),
                     reads=[("ps", 6), "R1"], writes=["R2"])
                p.op("dve", lambda e, yv=yv: e.tensor_tensor(out=yv, in0=R2, in1=yv, op=ALU.mult),
                     reads=["R2", ("YT", 8 + h)], writes=[("YT", 8 + h)])

        if debug:
            fin.append(p.op("sp", lambda e: e.dma_start(out=dbg["yT"], in_=YT.rearrange("p k t -> p (k t)")),
                            reads=[("YT", i) for i in range(16)], dma_sem="dbg4"))

        p.barrier()
        RG = v32(O_LAT, 2048)
        GG = v32(O_LAT + 8192, 2048)
        BG = v32(O_LAT + 16384, 2048)
        GP = v32(O_TR + 0, 2048)
        p.op("sp", lambda e: e.dma_start(out=BG, in_=b_gate_d.partition_broadcast(128)), writes=["BG"], dma_sem="c2")
        p.op("sp", lambda e: e.dma_start(out=GP, in_=g_post_d.partition_broadcast(128)), writes=["GP"], dma_sem="c3")
        for kc in range(16):
            p.op("dve", lambda e, kc=kc: e.tensor_scalar(out=RG[:, kc * 128:(kc + 1) * 128], in0=IDENT_F,
                                                         scalar1=gateT[:, kc:kc + 1], scalar2=1.0,
                                                         op0=ALU.mult, op1=ALU.mult),
                 reads=["identf"] + [("gateT", kc // 2)], writes=[("RG", kc // 4)])
        for g in range(4):
            gs = slice(g * 512, (g + 1) * 512)
            p.op("pe", lambda e, g=g, gs=gs: e.matmul(ps[g][:, :], lhsT=ONES_F, rhs=RG[:, gs], start=True, stop=True),
                 reads=["onesf", ("RG", g)], writes=[("ps", g)])
            p.op("dve", lambda e, g=g, gs=gs: e.tensor_tensor(out=GG[:, gs], in0=ps[g][:, :], in1=BG[:, gs], op=ALU.add),
                 reads=[("ps", g), "BG"], writes=[("GG", g)])
            p.op("dve", lambda e, gs=gs: e.tensor_tensor(out=GG[:, gs], in0=GG[:, gs], in1=GP[:, gs], op=ALU.mult),
                 reads=[("GG", g), "GP"], writes=[("GG", g)])
        XR = [v32(O_TR + 8192 + s * 8192, 2048) for s in range(2)]
        RES = [v32(O_TR + 24576 + s * 8192, 2048) for s in range(2)]
        JK = v16(O_TR + 40960, 512)
        outs = []
        YTK = [("YT", i) for i in range(16)]
        for tb in range(16):
            s = tb % 2
            ts_ = slice(tb * 128, (tb + 1) * 128)
            p.op("sp", lambda e, s=s, ts_=ts_: e.dma_start(out=XR[s], in_=x_d[ts_, :]), writes=[("XR", s)],
                 dma_sem="xt%d" % s)
            for nb in range(4):
                b = 4 * s + nb
                for kc in range(16):
                    p.op("pe", lambda e, b=b, kc=kc, nb=nb, ts_=ts_: e.matmul(
                        ps[b][:, :], lhsT=YT[:, kc, ts_], rhs=WOUT[:, kc, nb * 512:(nb + 1) * 512],
                        start=(kc == 0), stop=(kc == 15)),
                        reads=YTK + [("WOUT", kc // 4)], writes=[("ps", b)])
            for nb in range(4):
                b = 4 * s + nb
                p.op("act", lambda e, b=b, s=s, nb=nb: e.activation(out=JK, in_=ps[b][:, :], func=AF.Square,
                                                                    accum_out=ssy[:, 4 * s + nb:4 * s + nb + 1]),
                     reads=[("ps", b)], writes=["JK", ("ssy", s, nb)])
            p.op("dve", lambda e, s=s: e.reduce_sum(out=ssy1[:, s:s + 1], in_=ssy[:, 4 * s:4 * s + 4], axis=AX.X),
                 reads=[("ssy", s, nb) for nb in range(4)], writes=[("ssy1", s)])
            p.op("act", lambda e, s=s: e.activation(out=rty[:, s:s + 1], in_=ssy1[:, s:s + 1], func=AF.Ln,
                                                    bias=epsb, scale=1.0 / D),
                 reads=[("ssy1", s), "eps"], writes=[("rty", s)])
            p.op("act", lambda e, s=s: e.activation(out=rstdy[:, s:s + 1], in_=rty[:, s:s + 1], func=AF.Exp, scale=-0.5),
                 reads=[("rty", s)], writes=[("rstdy", s)])
            for nb in range(4):
                b = 4 * s + nb
                ns = slice(nb * 512, (nb + 1) * 512)
                p.op("dve", lambda e, b=b, s=s, ns=ns: e.scalar_tensor_tensor(
                    out=RES[s][:, ns], in0=ps[b][:, :], scalar=rstdy[:, s:s + 1], in1=GG[:, ns],
                    op0=ALU.mult, op1=ALU.mult),
                    reads=[("ps", b), ("rstdy", s)] + [("GG", nb)], writes=[("RES", s, nb)])
            p.op("pool", lambda e, s=s: e.tensor_tensor(out=RES[s], in0=RES[s], in1=XR[s], op=ALU.add),
                 reads=[("RES", s, nb) for nb in range(4)] + [("XR", s)], writes=[("RES", s, nb) for nb in range(4)])
            outs.append(p.op("sp", lambda e, s=s, ts_=ts_: e.dma_start(out=out_d[ts_, :], in_=RES[s]),
                             reads=[("RES", s, nb) for nb in range(4)], dma_sem="out%d" % s))
        p.wait_only("sp", outs[-2:] + fin)
        p.assign()

        sems = {e: es.enter_context(nc.semaphore("s_" + e)) for e in ENGS}
        dsems = {n: es.enter_context(nc.semaphore("d_" + n)) for n in p.dma_names}
        block = es.enter_context(nc.Block())

        @block.tensor
        def _(e):
            p.emit_engine("pe", e, sems, dsems)

        @block.scalar
        def _(e):
            p.emit_engine("act", e, sems, dsems)

        @block.vector
        def _(e):
            p.emit_engine("dve", e, sems, dsems)

        @block.gpsimd
        def _(e):
            p.emit_engine("pool", e, sems, dsems)

        @block.sync
        def _(e):
            p.emit_engine("sp", e, sems, dsems)
    return nc


_NC_CACHE = {}


def kernel(x, c, w_ada, b_ada, g_pre, w_in, g_q_lora, w_uq, g_kv_lora, w_ukv, w_out, g_post):
    x = np.asarray(x, np.float32)
    c = np.asarray(c, np.float32)
    sh = _prep_shared(w_ada, b_ada, g_pre, w_in, g_q_lora, w_uq, g_kv_lora, w_ukv, w_out, g_post)
    if "nc" not in _NC_CACHE:
        _NC_CACHE["nc"] = build_nc()
    nc = _NC_CACHE["nc"]
    in_maps = []
    for b in range(8):
        m = dict(sh)
        m["x"] = np.ascontiguousarray(x[b])
        m["cT"] = np.ascontiguousarray(c[b].reshape(16, 128).T)
        in_maps.append(m)
    res = run_bass_kernel_spmd(nc, in_maps, core_ids=list(range(8)))
    return np.stack([np.asarray(r["out"], np.float32) for r in res.results], axis=0)
```
